# Optimizing a Trainium2 kernel written in Bass

```python
import math
import jax, jax.numpy as jnp
from jax import lax
import numpy as np

D_MODEL = 1024
BATCH = 16
SEQ = 2048
DEPTH = 1

D_MIX = D_MODEL
D_SSM = D_MIX // 2
D_ATTN = D_MIX // 2
SSM_GROUP_CH = 16
SSM_GROUPS = D_SSM // SSM_GROUP_CH
SSM_STATE = 64
HEAD_DIM = 64
N_HEADS = D_ATTN // HEAD_DIM
KV_HEADS = 2
Q_PER_KV = N_HEADS // KV_HEADS
WINDOW = 128
BLOCK = 128
D_PLE = 256
EPS = 1e-6

SPLIT_SIZES = (D_SSM, D_SSM, N_HEADS * HEAD_DIM, KV_HEADS * HEAD_DIM, KV_HEADS * HEAD_DIM, D_ATTN)
SPLIT_IDX = tuple(int(s) for s in np.cumsum(SPLIT_SIZES)[:-1])
D_IN = sum(SPLIT_SIZES)

kernel_name = "hymba_s5_swa_sink_alibi_layer"


def rms_norm(x, g):
    xf = x.astype(jnp.float32)
    y = xf * lax.rsqrt(jnp.mean(xf * xf, axis=-1, keepdims=True) + EPS)
    return (y * g.astype(jnp.float32)).astype(x.dtype)


def alibi_slopes(n_heads):
    return jnp.exp2(-8.0 * (jnp.arange(n_heads, dtype=jnp.float32) + 1.0) / n_heads)


def s5_branch(u, lam_re, lam_im, log_step, b_re, b_im, c_re, c_im, d, w_glu, b_glu):
    f32 = jnp.float32
    bsz, seq, _ = u.shape
    uf = u.astype(f32).reshape(bsz, seq, SSM_GROUPS, SSM_GROUP_CH)
    lam = lax.complex(lam_re.astype(f32), lam_im.astype(f32))
    step = jnp.exp(log_step.astype(f32))[:, None]
    lam_bar = jnp.exp(lam * step)
    b = lax.complex(b_re.astype(f32), b_im.astype(f32))
    b_bar = ((lam_bar - 1.0) / lam)[..., None] * b
    bu = jnp.einsum('blgp,gnp->blgn', uf.astype(jnp.complex64), b_bar)
    a = jnp.broadcast_to(lam_bar[None, None], (1, seq, SSM_GROUPS, SSM_STATE))

    def combine(left, right):
        a_l, b_l = left
        a_r, b_r = right
        return a_r * a_l, a_r * b_l + b_r

    _, states = lax.associative_scan(combine, (a, bu), axis=1)
    c = lax.complex(c_re.astype(f32), c_im.astype(f32))
    y = jnp.einsum('blgn,gpn->blgp', states, c).real \
        + d.astype(f32).reshape(SSM_GROUPS, SSM_GROUP_CH) * uf
    y = y.reshape(bsz, seq, D_SSM)
    g = jax.nn.gelu(y)
    out = g * jax.nn.sigmoid(g @ w_glu.astype(f32) + b_glu.astype(f32))
    return out.astype(u.dtype)


def swa_branch(q, k, v, sinks):
    f32 = jnp.float32
    bsz, seq = q.shape[:2]
    nb = seq // BLOCK
    qb = q.reshape(bsz, nb, BLOCK, KV_HEADS, Q_PER_KV, HEAD_DIM)

    def band(t):
        tb = t.reshape(bsz, nb, BLOCK, KV_HEADS, HEAD_DIM)
        prev = jnp.concatenate([jnp.zeros_like(tb[:, :1]), tb[:, :-1]], axis=1)
        return jnp.concatenate([prev, tb], axis=2)

    kb, vb = band(k), band(v)
    scale = 1.0 / math.sqrt(HEAD_DIM)
    scores = jnp.einsum('bnqkgd,bnskd->bnkgqs', qb, kb, preferred_element_type=f32) * scale
    q_idx = jnp.arange(BLOCK)[:, None]
    s_idx = jnp.arange(2 * BLOCK)[None, :]
    dist = q_idx + BLOCK - s_idx
    valid = (dist >= 0) & (dist < WINDOW)
    block_ok = (jnp.arange(nb)[:, None] > 0) | (jnp.arange(2 * BLOCK)[None, :] >= BLOCK)
    mask = valid[None, :, :] & block_ok[:, None, :]
    slopes = alibi_slopes(N_HEADS).reshape(KV_HEADS, Q_PER_KV)
    bias = -slopes[:, :, None, None] * dist.astype(f32)[None, None]
    scores = jnp.where(mask[None, :, None, None], scores + bias[None, None], -jnp.inf)
    sink = sinks.astype(f32).reshape(KV_HEADS, Q_PER_KV)[None, None, :, :, None, None]
    m = jnp.maximum(jnp.max(scores, axis=-1, keepdims=True), sink)
    e = jnp.exp(scores - m)
    probs = e / (jnp.sum(e, axis=-1, keepdims=True) + jnp.exp(sink - m))
    out = jnp.einsum('bnkgqs,bnskd->bnqkgd', probs.astype(v.dtype), vb)
    return out.reshape(bsz, seq, N_HEADS * HEAD_DIM)


def setup_inputs(seed: int = 0) -> dict:
    key = jax.random.key(seed)
    ks = jax.random.split(key, 24)
    f32 = jnp.float32
    nrm = lambda k, shape, s: jax.random.normal(k, shape, f32) * s
    x = jax.random.normal(ks[0], (BATCH, SEQ, D_MODEL), f32)
    p = jax.random.normal(ks[1], (DEPTH, BATCH, SEQ, D_PLE), f32)
    pre_norm_g = 1.0 + nrm(ks[2], (DEPTH, D_MODEL), 0.02)
    w_in = nrm(ks[3], (DEPTH, D_MODEL, D_IN), D_MODEL ** -0.5)
    n = jnp.arange(SSM_STATE, dtype=f32)
    ssm_lam_re = -0.5 * jnp.exp(nrm(ks[4], (DEPTH, SSM_GROUPS, SSM_STATE), 0.05))
    ssm_lam_im = jnp.pi * n[None, None, :] + nrm(ks[5], (DEPTH, SSM_GROUPS, SSM_STATE), 0.01)
    ssm_log_step = jax.random.uniform(ks[6], (DEPTH, SSM_GROUPS), f32, math.log(1e-3), math.log(1e-1))
    bs = (2.0 * SSM_GROUP_CH) ** -0.5
    ssm_b_re = nrm(ks[7], (DEPTH, SSM_GROUPS, SSM_STATE, SSM_GROUP_CH), bs)
    ssm_b_im = nrm(ks[8], (DEPTH, SSM_GROUPS, SSM_STATE, SSM_GROUP_CH), bs)
    cs = (2.0 * SSM_STATE) ** -0.5
    ssm_c_re = nrm(ks[9], (DEPTH, SSM_GROUPS, SSM_GROUP_CH, SSM_STATE), cs)
    ssm_c_im = nrm(ks[10], (DEPTH, SSM_GROUPS, SSM_GROUP_CH, SSM_STATE), cs)
    ssm_d = nrm(ks[11], (DEPTH, D_SSM), 1.0)
    ssm_w_glu = nrm(ks[12], (DEPTH, D_SSM, D_SSM), D_SSM ** -0.5)
    ssm_b_glu = nrm(ks[13], (DEPTH, D_SSM), 0.01)
    attn_sinks = nrm(ks[14], (DEPTH, N_HEADS), 1.0)
    w_out = nrm(ks[15], (DEPTH, D_MIX, D_MODEL), D_MIX ** -0.5)
    post_norm_g = 1.0 + nrm(ks[16], (DEPTH, D_MODEL), 0.02)
    pl_w_proj = nrm(ks[17], (DEPTH, D_PLE, D_MODEL), D_PLE ** -0.5)
    pl_w_gate = nrm(ks[18], (DEPTH, D_MODEL, D_MODEL), D_MODEL ** -0.5)
    pl_b_gate = nrm(ks[19], (DEPTH, D_MODEL), 0.01)
    return {"x": x, "p": p, "pre_norm_g": pre_norm_g, "w_in": w_in,
            "ssm_lam_re": ssm_lam_re, "ssm_lam_im": ssm_lam_im, "ssm_log_step": ssm_log_step,
            "ssm_b_re": ssm_b_re, "ssm_b_im": ssm_b_im, "ssm_c_re": ssm_c_re, "ssm_c_im": ssm_c_im,
            "ssm_d": ssm_d, "ssm_w_glu": ssm_w_glu, "ssm_b_glu": ssm_b_glu,
            "attn_sinks": attn_sinks, "w_out": w_out, "post_norm_g": post_norm_g,
            "pl_w_proj": pl_w_proj, "pl_w_gate": pl_w_gate, "pl_b_gate": pl_b_gate}


def reference(x, p, pre_norm_g, w_in, ssm_lam_re, ssm_lam_im, ssm_log_step, ssm_b_re, ssm_b_im,
              ssm_c_re, ssm_c_im, ssm_d, ssm_w_glu, ssm_b_glu, attn_sinks, w_out, post_norm_g,
              pl_w_proj, pl_w_gate, pl_b_gate):
    bsz, seq, _ = x.shape
    h = x
    for i in range(DEPTH):
        hn = rms_norm(h, pre_norm_g[i])
        proj = hn @ w_in[i]
        u_ssm, z_ssm, q, k, v, z_attn = jnp.split(proj, SPLIT_IDX, axis=-1)
        ssm_out = s5_branch(u_ssm, ssm_lam_re[i], ssm_lam_im[i], ssm_log_step[i],
                            ssm_b_re[i], ssm_b_im[i], ssm_c_re[i], ssm_c_im[i], ssm_d[i],
                            ssm_w_glu[i], ssm_b_glu[i]) * jax.nn.silu(z_ssm)
        attn_out = swa_branch(q.reshape(bsz, seq, N_HEADS, HEAD_DIM),
                              k.reshape(bsz, seq, KV_HEADS, HEAD_DIM),
                              v.reshape(bsz, seq, KV_HEADS, HEAD_DIM),
                              attn_sinks[i]) * jax.nn.silu(z_attn)
        mixed = jnp.concatenate([ssm_out, attn_out], axis=-1) @ w_out[i]
        h = h + rms_norm(mixed, post_norm_g[i])
        gate = jax.nn.sigmoid(h @ pl_w_gate[i] + pl_b_gate[i])
        h = h + gate * (p[i] @ pl_w_proj[i])
    return h
```

```python
import math
from contextlib import ExitStack

import numpy as np
import concourse.bass as bass
import concourse.mybir as mybir
from concourse.bass_utils import run_bass_kernel_spmd

F32 = mybir.dt.float32
BF16 = mybir.dt.bfloat16
AF = mybir.ActivationFunctionType
ALU = mybir.AluOpType
NCORES = 8
NBLK = 4
PI = math.pi
EPS = 1e-6
BIG = 1.0e6


class Buf:
    def __init__(self, name):
        self.name = name
        self.last_w = None
        self.readers = []


class Eng:
    def __init__(self, name, handle, sem):
        self.name, self.h, self.sem = name, handle, sem
        self.count = 0
        self.seen = {}
        self.prog = []


class FW:
    def __init__(self, nc, es):
        self.nc = nc
        self.es = es
        self.engs = {}
        for name, h in [("pe", nc.tensor), ("act", nc.scalar), ("dve", nc.vector),
                        ("pool", nc.gpsimd), ("sp", nc.sync)]:
            sem = es.enter_context(nc.semaphore("s_" + name))
            self.engs[name] = Eng(name, h, sem)
        self.dsems = []

    def _waits(self, eng, reads, writes):
        deps = {}

        def add(d, raw):
            if d is None:
                return
            key, sem, val = d
            if key == eng.name and eng.name in ("pe", "sp"):
                return
            if key not in deps or deps[key][1] < val:
                deps[key] = (sem, val)
        for b in reads:
            add(b.last_w, True)
        for b in writes:
            add(b.last_w, False)
            for r in b.readers:
                add(r, False)
        out = []
        for key, (sem, val) in deps.items():
            if eng.seen.get(key, 0) >= val:
                continue
            eng.seen[key] = val
            out.append((sem, val))
        return out

    def op(self, engname, fn, reads=(), writes=()):
        eng = self.engs[engname]
        waits = self._waits(eng, reads, writes)
        eng.count += 1
        cnt = eng.count
        sem = eng.sem
        h = eng.h

        def emit():
            for (s, v) in waits:
                h.wait_ge(s, v)
            fn(h).then_inc(sem, 1)
        eng.prog.append(emit)
        d = (eng.name, sem, cnt)
        for b in writes:
            b.last_w = d
            b.readers = []
        for b in reads:
            if b not in writes:
                b.readers.append(d)

    def dsem(self, name):
        d = {"sem": self.es.enter_context(self.nc.semaphore("d_" + name)), "val": 0, "key": "dma_" + name}
        self.dsems.append(d)
        return d

    def dma(self, out, in_, dsem, reads=(), writes=(), engname="sp"):
        eng = self.engs[engname]
        waits = [w for w in self._waits(eng, reads, writes) if w[0] is not dsem["sem"]]
        dsem["val"] += 16
        val = dsem["val"]
        s = dsem["sem"]
        h = eng.h

        def emit():
            for (s2, v) in waits:
                h.wait_ge(s2, v)
            h.dma_start(out=out, in_=in_).then_inc(s, 16)
        eng.prog.append(emit)
        d = (dsem["key"], s, val)
        for b in writes:
            b.last_w = d
            b.readers = []
        for b in reads:
            b.readers.append(d)

    def barrier(self):
        targets = [(e.name, e.sem, e.count) for e in self.engs.values() if e.count > 0]
        targets += [(d["key"], d["sem"], d["val"]) for d in self.dsems if d["val"] > 0]
        for eng in self.engs.values():
            lst = []
            for key, sem, val in targets:
                if key == eng.name:
                    continue
                if eng.seen.get(key, 0) >= val:
                    continue
                eng.seen[key] = val
                lst.append((sem, val))
            h = eng.h

            def emit(lst=lst, h=h):
                for (s, v) in lst:
                    h.wait_ge(s, v)
            eng.prog.append(emit)

    def emit_all(self):
        nc = self.nc
        with nc.Block() as block:
            @block.tensor
            def _(e):
                for f in self.engs["pe"].prog:
                    f()

            @block.scalar
            def _(e):
                for f in self.engs["act"].prog:
                    f()

            @block.vector
            def _(e):
                for f in self.engs["dve"].prog:
                    f()

            @block.gpsimd
            def _(e):
                for f in self.engs["pool"].prog:
                    f()

            @block.sync
            def _(e):
                for f in self.engs["sp"].prog:
                    f()


class Arena:
    def __init__(self, ap_bf16, nbytes):
        self.ap = ap_bf16
        self.nbytes = nbytes
        self.off = 0

    def reset(self):
        self.off = 0

    def carve(self, shape, dt):
        esz = 4 if dt == F32 else 2
        n = 1
        for s in shape[1:]:
            n *= s
        nb = n * esz
        nb_al = (nb + 63) // 64 * 64
        assert self.off + nb_al <= self.nbytes, ("arena overflow", self.off, nb_al, self.nbytes)
        v = self.ap[0:shape[0], self.off // 2:(self.off + nb) // 2]
        self.off += nb_al
        if dt == F32:
            v = v.bitcast(F32)
        if len(shape) == 3:
            v = v.rearrange("p (a b) -> p a b", a=shape[1])
        elif len(shape) == 4:
            v = v.rearrange("p (a b c) -> p a b c", a=shape[1], b=shape[2])
        return v


def build_nc(taps=None):
    nc = bass.Bass("TRN2", target_bir_lowering=False)

    def din(name, shape, dt=F32):
        return nc.dram_tensor(name, shape, dt, kind="ExternalInput").ap()
    x_d = din("x", [4096, 1024])
    p_d = din("p", [4096, 256])
    win_d = din("w_in_r", [128, 18432])
    w5_d = din("w5", [128, 19456])
    wglu_d = din("wglu_r", [128, 2048])
    bglu_d = din("bglu_pad", [128, 512])
    gcol_d = din("gcol", [128, 8])
    gpost_d = din("gpost_t", [128, 1024])
    lamre_d = din("lamre2", [128, 32])
    lamim_d = din("lamim2", [128, 32])
    lstep_d = din("lstep", [128, 32])
    bA_d = din("bA", [128, 512])
    bB_d = din("bB", [128, 512])
    cA_d = din("cA", [128, 512])
    cB_d = din("cB", [128, 512])
    dcol_d = din("dcol", [128, 32])
    sinks_d = din("sinks_b", [128, 8])
    identf_d = din("identf", [128, 128])
    mcausal_d = din("mcausal", [128, 128])
    negtab_d = din("negtab", [128, 256])
    sgn_d = din("sgn", [128, 1])
    out_d = nc.dram_tensor("out", [4096, 1024], F32, kind="ExternalOutput").ap()
    scr_in = nc.dram_tensor("scr_in", [128, 18432], BF16, kind="Internal").ap()
    scr_p5 = nc.dram_tensor("scr_p5", [128, 19456], BF16, kind="Internal").ap()
    tap_d = {}
    if taps:
        for name, shape in taps.items():
            tap_d[name] = nc.dram_tensor("tap_" + name, list(shape), F32, kind="ExternalOutput").ap()

    xv = x_d.rearrange("(b i t) d -> b i t d", b=NBLK, i=128, t=8)
    pv = p_d.rearrange("(b i t) d -> b i t d", b=NBLK, i=128, t=8)
    ov = out_d.rearrange("(b i t) d -> b i t d", b=NBLK, i=128, t=8)

    with ExitStack() as es:
        fw = FW(nc, es)

        def sb(name, shape, dt):
            return es.enter_context(nc.sbuf_tensor(name, shape, dt))

        wbuf = sb("wbuf", [128, 19456], BF16)
        wglu = sb("wglu", [128, 4, 512], BF16)
        bglu = sb("bglu", [128, 512], BF16)
        RT = sb("RT", [128, 32, 128], BF16)
        TT = sb("TT", [128, 32, 128], BF16)
        OTb = sb("OTb", [128, 32, 128], BF16)
        gpost = sb("gpost", [128, 1024], F32)
        MD = sb("MD", [128, 2, 8, 128], BF16)
        identb = sb("identb", [128, 128], BF16)
        onesb = sb("onesb", [128, 128], BF16)
        esink = sb("esink", [128, 8], F32)
        AR = sb("AR", [128, 32], F32)
        AI = sb("AI", [128, 32], F32)
        AR2 = sb("AR2", [128, 32], F32)
        AI2 = sb("AI2", [128, 32], F32)
        AR4 = sb("AR4", [128, 32], F32)
        AI4 = sb("AI4", [128, 32], F32)
        AR8 = sb("AR8", [128, 32], F32)
        AI8 = sb("AI8", [128, 32], F32)
        AR16 = sb("AR16", [128, 32], F32)
        AI16 = sb("AI16", [128, 32], F32)
        carry = sb("carry", [128, 32], F32)
        ARENA_BYTES = 130 * 1024
        arena_t = sb("arena", [128, ARENA_BYTES // 2], BF16)
        arena = Arena(arena_t, ARENA_BYTES)
        psA = es.enter_context(nc.psum_tensor("psA", [128, 4, 512], F32))
        psB = es.enter_context(nc.psum_tensor("psB", [128, 2, 512], F32))
        psT = es.enter_context(nc.psum_tensor("psT", [128, 1024], BF16))
        psS = es.enter_context(nc.psum_tensor("psS", [128, 512], F32))
        B_psA = [Buf("psA%d" % i) for i in range(4)]
        B_psB = [Buf("psB%d" % i) for i in range(2)]
        B_psT = Buf("psT")
        psS16 = psS[:].bitcast(BF16)
        psA3_16 = psA[:, 3, :].bitcast(BF16)
        B_psS = Buf("psS")

        def tt(eng, out, in0, in1, op, reads, writes):
            fw.op(eng, lambda h: h.tensor_tensor(out=out, in0=in0, in1=in1, op=op), reads, writes)

        def ts(eng, out, in0, s1, s2, op0, op1, reads, writes):
            if op1 is None:
                fw.op(eng, lambda h: h.tensor_scalar(out=out, in0=in0, scalar1=s1, scalar2=None, op0=op0), reads, writes)
            else:
                fw.op(eng, lambda h: h.tensor_scalar(out=out, in0=in0, scalar1=s1, scalar2=s2, op0=op0, op1=op1), reads, writes)

        def stt(out, in0, scalar, in1, op0, op1, reads, writes):
            fw.op("dve", lambda h: h.scalar_tensor_tensor(out=out, in0=in0, scalar=scalar, in1=in1, op0=op0, op1=op1), reads, writes)

        def act(out, in_, func, reads, writes, scale=None, bias=None, accum=None):
            kw = {}
            if scale is not None:
                kw["scale"] = scale
            if bias is not None:
                kw["bias"] = bias
            if accum is not None:
                kw["accum_out"] = accum
            fw.op("act", lambda h: h.activation(out=out, in_=in_, func=func, **kw), reads, writes)

        def cp(eng, out, in_, reads, writes):
            if eng == "act":
                act(out, in_, AF.Copy, reads, writes)
            else:
                fw.op(eng, lambda h: h.tensor_copy(out=out, in_=in_), reads, writes)

        def mm(out, lhsT, rhs, start, stop, reads, writes):
            fw.op("pe", lambda h: h.matmul(out, lhsT=lhsT, rhs=rhs, start=start, stop=stop), reads, writes)

        def tr(out, in_, ident, reads, writes):
            fw.op("pe", lambda h: h.transpose(out=out, in_=in_, identity=ident), reads, writes)

        def memset(eng, ap, val, writes):
            fw.op(eng, lambda h: h.memset(ap, val), (), writes)

        d_s0 = fw.dsem("s0in")

        def load(dst, src, name, buf):
            fw.dma(dst, src, d_s0, writes=[buf])

        S0 = Buf("S0")
        S0in = Buf("S0in")
        B_res = Buf("res")
        identf = arena.carve([128, 128], F32)
        mcausal = arena.carve([128, 128], F32)
        negtab = arena.carve([128, 2, 128], F32)
        sgn = arena.carve([128, 1], F32)
        nsgn = arena.carve([128, 1], F32)
        lamre = arena.carve([128, 32], F32)
        lamim = arena.carve([128, 32], F32)
        lstep = arena.carve([128, 32], F32)
        bA = arena.carve([128, 32, 16], F32)
        bB = arena.carve([128, 32, 16], F32)
        cA = arena.carve([128, 32, 16], F32)
        cB = arena.carve([128, 32, 16], F32)
        dcol = arena.carve([128, 32], F32)
        sinks = arena.carve([128, 8], F32)
        gcol = arena.carve([128, 8], F32)
        bglu_f = arena.carve([128, 512], F32)
        for dst, src, nm in [(identf, identf_d, "identf"), (mcausal, mcausal_d, "mc"),
                             (negtab, negtab_d.rearrange("p (a b) -> p a b", a=2), "negtab"), (sgn, sgn_d, "sgn"),
                             (lamre, lamre_d, "lamre"), (lamim, lamim_d, "lamim"), (lstep, lstep_d, "lstep"),
                             (bA, bA_d.rearrange("p (a b) -> p a b", a=32), "bA"),
                             (bB, bB_d.rearrange("p (a b) -> p a b", a=32), "bB"),
                             (cA, cA_d.rearrange("p (a b) -> p a b", a=32), "cA"),
                             (cB, cB_d.rearrange("p (a b) -> p a b", a=32), "cB"),
                             (dcol, dcol_d, "dcol"), (sinks, sinks_d, "sinks"), (gcol, gcol_d, "gcol"),
                             (bglu_f, bglu_d, "bgluf")]:
            load(dst, src, nm, S0in)
        B_gpost = Buf("gpost")
        fw.dma(gpost[:], gpost_d, fw.dsem("gpost"), writes=[B_gpost])

        def s0(eng, fn):
            fw.op(eng, fn, [S0, S0in], [S0])

        def T32():
            return arena.carve([128, 32], F32)

        s0("dve", lambda h: h.tensor_copy(out=identb[:], in_=identf))
        s0("dve", lambda h: h.memset(onesb[:], 1.0))
        s0("dve", lambda h: h.tensor_copy(out=bglu[:], in_=bglu_f))
        s0("dve", lambda h: h.tensor_scalar(out=nsgn, in0=sgn, scalar1=-1.0, scalar2=None, op0=ALU.mult))
        s0("act", lambda h: h.activation(out=esink[:], in_=sinks, func=AF.Exp))
        for hh in range(8):
            slope = 2.0 ** (-(hh + 1))
            s0("act", lambda h, hh=hh, slope=slope: h.activation(out=MD[:, :, hh, :], in_=negtab, func=AF.Copy, scale=-8.0 * slope))
        step = T32(); a_ = T32(); th = T32(); mag = T32(); w_ = T32(); rs = T32(); sin1 = T32(); cos1 = T32(); thc = T32()
        s0("act", lambda h: h.activation(out=step, in_=lstep, func=AF.Exp))
        s0("dve", lambda h: h.tensor_tensor(out=a_, in0=lamre, in1=step, op=ALU.mult))
        s0("dve", lambda h: h.tensor_tensor(out=th, in0=lamim, in1=step, op=ALU.mult))
        s0("act", lambda h: h.activation(out=mag, in_=a_, func=AF.Exp))

        def wrap(dst, src):
            first = True
            for m in (1.0, 3.0, 5.0, 7.0):
                s0("dve", lambda h, m=m: h.tensor_scalar(out=w_, in0=src, scalar1=m * PI, scalar2=-2.0 * PI, op0=ALU.is_gt, op1=ALU.mult))
                if first:
                    s0("dve", lambda h: h.tensor_tensor(out=dst, in0=src, in1=w_, op=ALU.add))
                    first = False
                else:
                    s0("dve", lambda h: h.tensor_tensor(out=dst, in0=dst, in1=w_, op=ALU.add))
        wrap(rs, th)
        s0("act", lambda h: h.activation(out=sin1, in_=rs, func=AF.Sin))
        s0("dve", lambda h: h.tensor_scalar(out=thc, in0=th, scalar1=PI / 2.0, scalar2=None, op0=ALU.add))
        wrap(rs, thc)
        s0("act", lambda h: h.activation(out=cos1, in_=rs, func=AF.Sin))
        B_wbuf = Buf("wbuf")
        B_wb0 = Buf("wbuf0")
        B_stg = [Buf("stg0"), Buf("stg1"), Buf("stg2")]
        B_stgb = [Buf("stgb0"), Buf("stgb1")]
        stg = [arena.carve([128, 2304], F32), arena.carve([128, 2304], F32), arena.carve([128, 2304], F32)]
        stgb = [arena.carve([128, 2048], BF16), arena.carve([128, 2048], BF16)]
        d_stg = [fw.dsem("stg0"), fw.dsem("stg1"), fw.dsem("stg2")]
        d_scr = fw.dsem("scr")
        B_scr = Buf("scr")
        cast_engs = ["act", "act", "act", "act"]
        ci = 0
        for kt in range(8):
            sl = kt % 3
            fw.dma(stg[sl], win_d[:, kt * 2304:(kt + 1) * 2304], d_stg[sl], writes=[B_stg[sl]])
            eng = cast_engs[ci % 4]; ci += 1
            dst = wbuf[:, kt * 2304:(kt + 1) * 2304]
            if eng == "act":
                fw.op("act", lambda h, dst=dst, sl=sl, kt=kt: h.activation(out=dst, in_=stg[sl], func=AF.Copy, scale=gcol[:, kt:kt + 1]), [B_stg[sl], S0in], [B_wbuf, B_wb0])
            else:
                fw.op(eng, lambda h, dst=dst, sl=sl, kt=kt: h.tensor_scalar(out=dst, in0=stg[sl], scalar1=gcol[:, kt:kt + 1], scalar2=None, op0=ALU.mult), [B_stg[sl], S0in], [B_wbuf, B_wb0])
        d_stgb = [fw.dsem("stgb0"), fw.dsem("stgb1")]
        nch = (19456 + 2047) // 2048
        for c in range(nch):
            sl = (c + 2) % 3
            sb2 = c % 2
            c0 = c * 2048
            c1 = min(19456, c0 + 2048)
            n = c1 - c0
            fw.dma(stg[sl][:, 0:n], w5_d[:, c0:c1], d_stg[sl], writes=[B_stg[sl]])
            eng = cast_engs[ci % 4]; ci += 1
            cp(eng, stgb[sb2][:, 0:n], stg[sl][:, 0:n], [B_stg[sl]], [B_stgb[sb2]])
            fw.dma(scr_p5[:, c0:c1], stgb[sb2][:, 0:n], d_stgb[sb2], reads=[B_stgb[sb2]])
        sl = (nch + 2) % 3
        fw.dma(stg[sl][:, 0:2048], wglu_d[:, :], d_stg[sl], writes=[B_stg[sl]])
        B_wglu = Buf("wglu")
        cp("act", wglu[:].rearrange("p a b -> p (a b)"), stg[sl][:, 0:2048], [B_stg[sl]], [B_wglu])

        Pre = arena.carve([128, 32, 9], F32)
        Pim = arena.carve([128, 32, 9], F32)
        s0("dve", lambda h: h.memset(Pre[:, :, 0:1], 1.0))
        s0("dve", lambda h: h.memset(Pim[:, :, 0:1], 0.0))
        s0("dve", lambda h: h.tensor_tensor(out=Pre[:, :, 1], in0=mag, in1=cos1, op=ALU.mult))
        s0("dve", lambda h: h.tensor_tensor(out=Pim[:, :, 1], in0=mag, in1=sin1, op=ALU.mult))
        ta = arena.carve([128, 32, 8], F32)
        tb = arena.carve([128, 32, 8], F32)

        def cmul(ore, oim, are, aim, bre, bim, shape):
            k = shape[2]
            bre_b = bre.to_broadcast(shape) if bre.shape[2] == 1 and k > 1 else bre
            bim_b = bim.to_broadcast(shape) if bim.shape[2] == 1 and k > 1 else bim
            s0("dve", lambda h: h.tensor_tensor(out=ta[:, :, 0:k], in0=are, in1=bre_b, op=ALU.mult))
            s0("dve", lambda h: h.tensor_tensor(out=tb[:, :, 0:k], in0=aim, in1=bim_b, op=ALU.mult))
            s0("dve", lambda h: h.tensor_tensor(out=ore, in0=ta[:, :, 0:k], in1=tb[:, :, 0:k], op=ALU.subtract))
            s0("dve", lambda h: h.tensor_tensor(out=ta[:, :, 0:k], in0=are, in1=bim_b, op=ALU.mult))
            s0("dve", lambda h: h.tensor_tensor(out=tb[:, :, 0:k], in0=aim, in1=bre_b, op=ALU.mult))
            s0("dve", lambda h: h.tensor_tensor(out=oim, in0=ta[:, :, 0:k], in1=tb[:, :, 0:k], op=ALU.add))
        for k in (1, 2, 4):
            cmul(Pre[:, :, k + 1:2 * k + 1], Pim[:, :, k + 1:2 * k + 1], Pre[:, :, 1:k + 1], Pim[:, :, 1:k + 1],
                 Pre[:, :, k:k + 1], Pim[:, :, k:k + 1], [128, 32, k])
        m2 = arena.carve([128, 32, 1], F32); inv = arena.carve([128, 32, 1], F32)
        pm8re = arena.carve([128, 32, 1], F32); pm8im = arena.carve([128, 32, 1], F32)
        t1a = arena.carve([128, 32, 1], F32)
        P8re = Pre[:, :, 8:9]; P8im = Pim[:, :, 8:9]
        s0("dve", lambda h: h.tensor_tensor(out=m2, in0=P8re, in1=P8re, op=ALU.mult))
        s0("dve", lambda h: h.tensor_tensor(out=t1a, in0=P8im, in1=P8im, op=ALU.mult))
        s0("dve", lambda h: h.tensor_tensor(out=m2, in0=m2, in1=t1a, op=ALU.add))
        s0("dve", lambda h: h.reciprocal(out=inv, in_=m2))
        s0("dve", lambda h: h.tensor_tensor(out=pm8re, in0=P8re, in1=inv, op=ALU.mult))
        s0("dve", lambda h: h.tensor_tensor(out=pm8im, in0=P8im, in1=inv, op=ALU.mult))
        s0("dve", lambda h: h.tensor_scalar(out=pm8im, in0=pm8im, scalar1=-1.0, scalar2=None, op0=ALU.mult))
        nre = arena.carve([128, 32, 1], F32); den = arena.carve([128, 32, 1], F32)
        cre = arena.carve([128, 32, 1], F32); cim = arena.carve([128, 32, 1], F32)
        lre3 = lamre.unsqueeze(2); lim3 = lamim.unsqueeze(2)
        P1re = Pre[:, :, 1:2]; P1im = Pim[:, :, 1:2]
        s0("dve", lambda h: h.tensor_scalar(out=nre, in0=P1re, scalar1=-1.0, scalar2=None, op0=ALU.add))
        s0("dve", lambda h: h.tensor_tensor(out=den, in0=lre3, in1=lre3, op=ALU.mult))
        s0("dve", lambda h: h.tensor_tensor(out=t1a, in0=lim3, in1=lim3, op=ALU.mult))
        s0("dve", lambda h: h.tensor_tensor(out=den, in0=den, in1=t1a, op=ALU.add))
        s0("dve", lambda h: h.reciprocal(out=inv, in_=den))
        s0("dve", lambda h: h.tensor_tensor(out=cre, in0=nre, in1=lre3, op=ALU.mult))
        s0("dve", lambda h: h.tensor_tensor(out=t1a, in0=P1im, in1=lim3, op=ALU.mult))
        s0("dve", lambda h: h.tensor_tensor(out=cre, in0=cre, in1=t1a, op=ALU.add))
        s0("dve", lambda h: h.tensor_tensor(out=cre, in0=cre, in1=inv, op=ALU.mult))
        s0("dve", lambda h: h.tensor_tensor(out=cim, in0=P1im, in1=lre3, op=ALU.mult))
        s0("dve", lambda h: h.tensor_tensor(out=t1a, in0=nre, in1=lim3, op=ALU.mult))
        s0("dve", lambda h: h.tensor_tensor(out=cim, in0=cim, in1=t1a, op=ALU.subtract))
        s0("dve", lambda h: h.tensor_tensor(out=cim, in0=cim, in1=inv, op=ALU.mult))
        Fre = arena.carve([128, 32, 8], F32); Fim = arena.carve([128, 32, 8], F32); F2 = arena.carve([128, 32, 8], F32)
        Gre = arena.carve([128, 32, 8], F32); Gim = arena.carve([128, 32, 8], F32); G2 = arena.carve([128, 32, 8], F32)
        cmul(Fre, Fim, Pre[:, :, 0:8], Pim[:, :, 0:8], cre, cim, [128, 32, 8])
        cmul(Gre, Gim, Fre, Fim, pm8re, pm8im, [128, 32, 8])
        s0("dve", lambda h: h.tensor_scalar(out=F2, in0=Fim, scalar1=sgn, scalar2=None, op0=ALU.mult))
        s0("dve", lambda h: h.tensor_scalar(out=G2, in0=Gim, scalar1=sgn, scalar2=None, op0=ALU.mult))
        p16re = arena.carve([128, 32, 1], F32); p16im = arena.carve([128, 32, 1], F32)
        s0("dve", lambda h: h.tensor_tensor(out=p16re, in0=P8re, in1=P8re, op=ALU.mult))
        s0("dve", lambda h: h.tensor_tensor(out=t1a, in0=P8im, in1=P8im, op=ALU.mult))
        s0("dve", lambda h: h.tensor_tensor(out=p16re, in0=p16re, in1=t1a, op=ALU.subtract))
        s0("dve", lambda h: h.tensor_tensor(out=p16im, in0=P8re, in1=P8im, op=ALU.mult))
        s0("dve", lambda h: h.tensor_scalar(out=p16im, in0=p16im, scalar1=2.0, scalar2=None, op0=ALU.mult))
        p32re = arena.carve([128, 32, 1], F32); p32im = arena.carve([128, 32, 1], F32)
        s0("dve", lambda h: h.tensor_tensor(out=p32re, in0=p16re, in1=p16re, op=ALU.mult))
        s0("dve", lambda h: h.tensor_tensor(out=t1a, in0=p16im, in1=p16im, op=ALU.mult))
        s0("dve", lambda h: h.tensor_tensor(out=p32re, in0=p32re, in1=t1a, op=ALU.subtract))
        s0("dve", lambda h: h.tensor_tensor(out=p32im, in0=p16re, in1=p16im, op=ALU.mult))
        s0("dve", lambda h: h.tensor_scalar(out=p32im, in0=p32im, scalar1=2.0, scalar2=None, op0=ALU.mult))
        p64re = arena.carve([128, 32, 1], F32); p64im = arena.carve([128, 32, 1], F32)
        s0("dve", lambda h: h.tensor_tensor(out=p64re, in0=p32re, in1=p32re, op=ALU.mult))
        s0("dve", lambda h: h.tensor_tensor(out=t1a, in0=p32im, in1=p32im, op=ALU.mult))
        s0("dve", lambda h: h.tensor_tensor(out=p64re, in0=p64re, in1=t1a, op=ALU.subtract))
        s0("dve", lambda h: h.tensor_tensor(out=p64im, in0=p32re, in1=p32im, op=ALU.mult))
        s0("dve", lambda h: h.tensor_scalar(out=p64im, in0=p64im, scalar1=2.0, scalar2=None, op0=ALU.mult))
        p128re = arena.carve([128, 32, 1], F32); p128im = arena.carve([128, 32, 1], F32)
        s0("dve", lambda h: h.tensor_tensor(out=p128re, in0=p64re, in1=p64re, op=ALU.mult))
        s0("dve", lambda h: h.tensor_tensor(out=t1a, in0=p64im, in1=p64im, op=ALU.mult))
        s0("dve", lambda h: h.tensor_tensor(out=p128re, in0=p128re, in1=t1a, op=ALU.subtract))
        s0("dve", lambda h: h.tensor_tensor(out=p128im, in0=p64re, in1=p64im, op=ALU.mult))
        s0("dve", lambda h: h.tensor_scalar(out=p128im, in0=p128im, scalar1=2.0, scalar2=None, op0=ALU.mult))
        s0("dve", lambda h: h.tensor_copy(out=AR16[:], in_=p128re[:, :, 0]))
        s0("dve", lambda h: h.tensor_scalar(out=AI16[:], in0=p128im[:, :, 0], scalar1=nsgn, scalar2=None, op0=ALU.mult))
        s0("dve", lambda h: h.tensor_copy(out=AR8[:], in_=p64re[:, :, 0]))
        s0("dve", lambda h: h.tensor_scalar(out=AI8[:], in0=p64im[:, :, 0], scalar1=nsgn, scalar2=None, op0=ALU.mult))
        s0("dve", lambda h: h.tensor_copy(out=AR4[:], in_=p32re[:, :, 0]))
        s0("dve", lambda h: h.tensor_scalar(out=AI4[:], in0=p32im[:, :, 0], scalar1=nsgn, scalar2=None, op0=ALU.mult))
        s0("dve", lambda h: h.tensor_copy(out=AR2[:], in_=p16re[:, :, 0]))
        s0("dve", lambda h: h.tensor_scalar(out=AI2[:], in0=p16im[:, :, 0], scalar1=nsgn, scalar2=None, op0=ALU.mult))
        s0("dve", lambda h: h.tensor_copy(out=AR[:], in_=Pre[:, :, 8]))
        s0("dve", lambda h: h.tensor_scalar(out=AI[:], in0=Pim[:, :, 8], scalar1=nsgn, scalar2=None, op0=ALU.mult))
        R = arena.carve([128, 32, 8, 16], F32)
        R2 = arena.carve([128, 32, 8, 16], F32)
        OT = arena.carve([128, 32, 8, 16], F32)
        t512a = arena.carve([128, 32, 16], F32)
        t512b = arena.carve([128, 32, 16], F32)
        S0R = Buf("S0R"); S0T = Buf("S0T"); S0P = Buf("S0P"); B_tmp4 = Buf("tmp4"); B_TTb = Buf("TTb"); B_RTb = Buf("RTb")
        t512c = arena.carve([128, 32, 16], F32)
        t512d = arena.carve([128, 32, 16], F32)
        for s_ in range(8):
            k = 7 - s_
            fw.op("dve", lambda h, k=k: h.tensor_tensor(out=t512a, in0=bA, in1=Fre[:, :, k:k + 1].to_broadcast([128, 32, 16]), op=ALU.mult), [S0, S0in, S0T], [S0T])
            fw.op("dve", lambda h, k=k: h.tensor_tensor(out=t512b, in0=bB, in1=F2[:, :, k:k + 1].to_broadcast([128, 32, 16]), op=ALU.mult), [S0, S0in, S0T], [S0T])
            fw.op("dve", lambda h, s_=s_: h.tensor_tensor(out=R[:, :, s_, :], in0=t512a, in1=t512b, op=ALU.add), [S0T], [S0R])
        Rv = R.rearrange("p g s q -> p g (s q)")
        for g4 in range(8):
            pbk, Bp = (psB[:, 0, :], B_psB[0]) if g4 % 2 == 0 else (psB[:, 1, :], B_psB[1])
            for j in range(4):
                g = 4 * g4 + j
                fw.op("pe", lambda h, g=g, j=j, pbk=pbk: h.matmul(pbk[:, j * 128:(j + 1) * 128], lhsT=Rv[:, g, :], rhs=identf, start=True, stop=True), [S0R, S0in], [Bp])
            fw.op("act", lambda h, g4=g4, pbk=pbk: h.activation(out=RT[:, 4 * g4:4 * g4 + 4, :], in_=pbk.rearrange("p (a b) -> p a b", a=4), func=AF.Copy), [Bp], [B_RTb])
        G1 = arena.carve([128, 32, 8], F32); G2o = arena.carve([128, 32, 8], F32)
        S0O = Buf("S0O")
        fw.op("dve", lambda h: h.tensor_scalar(out=G1, in0=Pre[:, :, 1:9], scalar1=nsgn, scalar2=None, op0=ALU.mult), [S0, S0in], [S0O])
        fw.op("dve", lambda h: h.tensor_scalar(out=G2o, in0=Pim[:, :, 1:9], scalar1=-1.0, scalar2=None, op0=ALU.mult), [S0, S0in, S0O], [S0O])
        for s_ in range(8):
            k = 7 - s_
            fw.op("pool", lambda h, k=k: h.tensor_tensor(out=t512c, in0=bA, in1=Gre[:, :, k:k + 1].to_broadcast([128, 32, 16]), op=ALU.mult), [S0, S0in, S0P], [S0P])
            fw.op("pool", lambda h, k=k: h.tensor_tensor(out=t512d, in0=bB, in1=G2[:, :, k:k + 1].to_broadcast([128, 32, 16]), op=ALU.mult), [S0, S0in, S0P], [S0P])
            fw.op("pool", lambda h, s_=s_: h.tensor_tensor(out=R2[:, :, s_, :], in0=t512c, in1=t512d, op=ALU.add), [S0P], [S0P])
        for t in range(8):
            fw.op("dve", lambda h, t=t: h.tensor_tensor(out=t512a, in0=cA, in1=G1[:, :, t:t + 1].to_broadcast([128, 32, 16]), op=ALU.mult), [S0in, S0O, S0T], [S0T])
            fw.op("dve", lambda h, t=t: h.tensor_tensor(out=t512b, in0=cB, in1=G2o[:, :, t:t + 1].to_broadcast([128, 32, 16]), op=ALU.mult), [S0in, S0O, S0T], [S0T])
            fw.op("dve", lambda h, t=t: h.tensor_tensor(out=OT[:, :, t, :], in0=t512a, in1=t512b, op=ALU.add), [S0T], [S0O])
        Rv = R.rearrange("p g s q -> p g (s q)")
        R2v = R2.rearrange("p g s q -> p g (s q)")
        OTv = OT.rearrange("p g s q -> p g (s q)")
        fw.op("act", lambda h: h.activation(out=OTb[:], in_=OTv, func=AF.Copy), [S0O], [B_TTb])
        tmp4 = [arena.carve([128, 4, 128], F32), arena.carve([128, 4, 128], F32)]
        B_tmp4s = [B_tmp4, Buf("tmp4b")]
        for g4 in range(8):
            pbk, Bp = (psS[:], B_psS) if g4 % 2 == 0 else (psA[:, 0, :], B_psA[0])
            tm, Btm = tmp4[g4 % 2], B_tmp4s[g4 % 2]
            for j in range(4):
                g = 4 * g4 + j
                fw.op("pe", lambda h, g=g, j=j, pbk=pbk: h.matmul(pbk[:, j * 128:(j + 1) * 128], lhsT=R2v[:, g, :], rhs=OTv[:, g, :], start=True, stop=True), [S0P, S0O], [Bp])
            fw.op("dve", lambda h, pbk=pbk, tm=tm: h.tensor_tensor(out=tm, in0=pbk.rearrange("p (a b) -> p a b", a=4), in1=mcausal.unsqueeze(1).to_broadcast([128, 4, 128]), op=ALU.mult), [Bp, S0in], [Btm])
            for j in range(4):
                g = 4 * g4 + j
                fw.op("dve", lambda h, g=g, j=j, tm=tm: h.scalar_tensor_tensor(out=TT[:, g, :], in0=identf, scalar=dcol[:, g:g + 1], in1=tm[:, j, :], op0=ALU.mult, op1=ALU.add), [Btm, S0in], [B_TTb])

        fw.barrier()

        arena.reset()
        bigT = arena.carve([128, 8, 1024], BF16)
        uSM = arena.carve([128, 4096], BF16)
        zs = arena.carve([128, 8, 512], BF16)
        za = arena.carve([128, 8, 512], BF16)
        QT = arena.carve([128, 4, 1024], BF16)
        KT = arena.carve([128, 4, 1152], BF16)
        V = arena.carve([128, 9, 2, 65], BF16)
        UgT = arena.carve([128, 32, 128], BF16)
        rb = arena.carve([128, 32, 128], BF16)
        Xbf = arena.carve([128, 32, 128], BF16)
        ring = [arena.carve([128, 32, 16], F32), arena.carve([128, 32, 16], F32)]
        st1 = arena.carve([128, 32], F32)
        st2 = arena.carve([128, 32], F32)
        E = arena.carve([128, 2, 8, 128], BF16)
        PT = arena.carve([128, 2, 8, 128], BF16)
        xs = [arena.carve([128, 1024], F32), arena.carve([128, 1024], F32), E.rearrange("p a b c -> p (a b c)").bitcast(F32),
              Xbf.rearrange("p a b -> p (a b)")[:, 0:2048].bitcast(F32)]
        hn = [arena.carve([128, 1024], BF16), arena.carve([128, 1024], BF16)]
        tmpg = arena.carve([128, 1024], F32)
        hTs = arena.carve([128, 8, 128], BF16)
        pslab = [arena.carve([128, 256], F32), arena.carve([128, 256], F32)]
        pb = [arena.carve([128, 256], BF16), arena.carve([128, 256], BF16)]
        tmpA = arena.carve([128, 1024], F32)
        pTs = arena.carve([128, 2, 128], BF16)
        gTs = [arena.carve([128, 4, 128], BF16), arena.carve([128, 4, 128], BF16)]
        sig = [arena.carve([128, 512], BF16), arena.carve([128, 512], BF16)]
        so = [arena.carve([128, 512], BF16), arena.carve([128, 512], BF16)]
        ao = arena.carve([128, 8, 64], BF16)
        ao2 = arena.carve([128, 512], BF16)
        den8 = arena.carve([128, 8], F32)
        rden8 = arena.carve([128, 8], F32)
        ss = arena.carve([128, 8], F32)
        rstd = arena.carve([128, 8], F32)
        ss2 = arena.carve([128, 8], F32)
        rstd2 = arena.carve([128, 8], F32)

        B_bigT = Buf("bigT"); B_bigTh = [Buf("bigT_lo"), Buf("bigT_hi")]; B_uSM = Buf("uSM"); B_zs = Buf("zs"); B_za = Buf("za"); B_QT = Buf("QT")
        B_KT = Buf("KT"); B_V = Buf("V"); B_UgT = Buf("UgT"); B_rb = Buf("rb"); B_Xbf = Buf("Xbf")
        B_ring = [Buf("ring0"), Buf("ring1")]; B_st1 = Buf("st1"); B_st2 = Buf("st2"); B_carry = Buf("carry")
        B_Ek = [Buf("E0"), Buf("E1")]; B_PTk = [Buf("PT0"), Buf("PT1")]; B_denk = [Buf("den0"), Buf("den1")]; B_aok = [Buf("ao0"), Buf("ao1")]; B_xs = [Buf("xs0"), Buf("xs1"), B_Ek[0], B_Xbf]; B_hn = [Buf("hn0"), Buf("hn1")]
        B_junk = Buf("junk"); B_tmpg = Buf("tmpg"); B_hTs = Buf("hTs"); B_ps = [Buf("pslab0"), Buf("pslab1")]
        B_pb = [Buf("pb0"), Buf("pb1")]; B_tmpA = Buf("tmpA"); B_pTs = Buf("pTs"); B_gTs = [Buf("gTs0"), Buf("gTs1")]; B_sig = [Buf("sig0"), Buf("sig1")]; B_so = [Buf("so0"), Buf("so1")]
        B_ao = Buf("ao"); B_ao2 = Buf("ao2"); B_den = Buf("den"); B_ss = Buf("ss"); B_rstd = Buf("rstd")
        B_ss2 = Buf("ss2"); B_rstd2 = Buf("rstd2")
        d_xs = [fw.dsem("xs0"), fw.dsem("xs1"), fw.dsem("xs2"), fw.dsem("xs3")]
        d_ps = [fw.dsem("ps0"), fw.dsem("ps1")]
        d_out = [fw.dsem("out0"), fw.dsem("out1"), fw.dsem("out2"), fw.dsem("out3")]
        d_wb = fw.dsem("wb")
        d_wb0 = fw.dsem("wb0")

        mhalf = arena.carve([128, 8], F32)
        B_mhalf = Buf("mhalf")
        memset("pool", mhalf, -0.5, [B_mhalf])
        memset("pool", KT, 0.0, [B_KT])
        memset("pool", V, 0.0, [B_V])
        memset("pool", V[:, 1:9, :, 64:65], 1.0, [B_V])
        memset("pool", carry[:], 0.0, [B_carry])

        WU, WZS, WQ, WK, WV, WZA = 0, 512, 1024, 1536, 1664, 1792
        WOUT, WGATE, WPROJ, WBG = 0, 8192, 16384, 18432

        def win(kt, c0, n):
            return wbuf[:, kt * 2304 + c0: kt * 2304 + c0 + n]

        evac_rr = [0]

        def evac_eng():
            evac_rr[0] += 1
            return "act" if evac_rr[0] % 2 else "dve"

        xslot = [0]

        p1slot = {}
        p1_early = set()

        def p1_A(b, t):
            xl = xslot[0] % 4; xslot[0] += 1
            sl = t % 2
            p1slot[(b, t)] = sl
            fw.dma(xs[xl], x_d[b * 1024 + t * 128:b * 1024 + (t + 1) * 128, :], d_xs[xl], writes=[B_xs[xl]])
            act(hn[sl], xs[xl], AF.Square, [B_xs[xl]], [B_hn[sl], B_ss], accum=ss[:, t:t + 1])
            ts("dve", rstd[:, t:t + 1], ss[:, t:t + 1], 1.0 / 1024.0, EPS, ALU.mult, ALU.add, [B_ss], [B_rstd])
            tt("pool", rstd[:, t:t + 1], rstd[:, t:t + 1], mhalf[:, 0:1], ALU.pow, [B_rstd, B_mhalf], [B_rstd])
            ts("dve", hn[sl], xs[xl], rstd[:, t:t + 1], None, ALU.mult, None, [B_xs[xl], B_rstd], [B_hn[sl]])

        for blk in range(NBLK):
            half = blk % 2
            if half == 0:
                memset("pool", KT[:, :, 0:128], 0.0, [B_KT])
                memset("pool", V[:, 0, :, :], 0.0, [B_V])
                memset("pool", carry[:], 0.0, [B_carry])
            else:
                cp("pool", KT[:, :, 0:128], KT[:, :, 1024:1152], [B_KT], [B_KT])
                cp("pool", V[:, 0, :, :], V[:, 8, :, :], [B_V], [B_V])
            def p1_B(t):
                sl = p1slot[(blk, t)]
                pst, Bp = (psT, B_psT) if t % 2 == 0 else (psS16, B_psS)
                for kt in range(8):
                    tr(pst[:, kt * 128:(kt + 1) * 128], hn[sl][:, kt * 128:(kt + 1) * 128], identb[:], [B_hn[sl], S0], [Bp])
                cp("act" if t % 2 == 0 else "dve", bigT[:, :, t * 128:(t + 1) * 128], pst[:].rearrange("p (a b) -> p a b", a=8), [Bp], [B_bigTh[0], B_bigTh[1]])
            if (blk, 0) not in p1_early:
                p1_A(blk, 0)
            for t in range(8):
                if t + 1 < 8 and (blk, t + 1) not in p1_early:
                    p1_A(blk, t + 1)
                if blk == 0:
                    c0 = 2304 * t
                    fw.dma(scr_in[:, c0:c0 + 2304], wbuf[:, c0:c0 + 2304], d_scr, reads=[B_wb0 if t < 3 else B_wbuf], writes=[B_scr])
                if blk > 0 and t < 5:
                    c0 = 6912 + 2304 * t
                    fw.dma(wbuf[:, c0:c0 + 2304], scr_in[:, c0:c0 + 2304], d_wb, reads=[B_scr], writes=[B_wbuf])
                p1_B(t)
            rr = [0]

            def bank():
                rr[0] += 1
                return rr[0] % 4
            for t in range(8):
                bk = bank()
                for kt in range(8):
                    mm(psA[:, bk, :], bigT[:, kt, t::8], win(kt, WU, 512), kt == 0, kt == 7, [B_bigTh[0], B_bigTh[1], (B_wb0 if kt < 3 else B_wbuf)], [B_psA[bk]])
                uv = uSM.rearrange("p (g t c) -> p g t c", g=32, t=8)
                cp(evac_eng(), uv[:, :, t, :], psA[:, bk, :].rearrange("p (g c) -> p g c", g=32), [B_psA[bk]], [B_uSM])
            uv2 = uSM.rearrange("p (g x) -> p g x", g=32)
            for g4 in range(8):
                pst, Bp = (psT, B_psT) if g4 % 2 == 0 else (psS16, B_psS)
                for j in range(4):
                    g = 4 * g4 + j
                    tr(pst[:, j * 128:(j + 1) * 128], uv2[:, g, :], identb[:], [B_uSM, S0], [Bp])
                cp(evac_eng(), UgT[:, 4 * g4:4 * g4 + 4, :], pst[:, 0:512].rearrange("p (a b) -> p a b", a=4), [Bp], [B_UgT])
            for g4 in range(8):
                pbk, Bp = (psB[:, 0, :], B_psB[0]) if g4 % 2 == 0 else (psB[:, 1, :], B_psB[1])
                for j in range(4):
                    g = 4 * g4 + j
                    mm(pbk[:, j * 128:(j + 1) * 128], RT[:, g, :], UgT[:, g, :], True, True, [S0, B_UgT], [Bp])
                cp(evac_eng(), rb[:, 4 * g4:4 * g4 + 4, :], pbk.rearrange("p (a b) -> p a b", a=4), [Bp], [B_rb])
            SE = "dve"

            r2 = tmpA.bitcast(BF16).rearrange("p (g c) -> p g c", g=32)
            tf = tmpg.rearrange("p (k g c) -> p k g c", k=2, g=32)

            def cmul_A(eng, dst, src_lohi, addend, ARt, AIt, Bsrc, Bdst, w=16):
                ARb = ARt[:].unsqueeze(2).to_broadcast([128, 32, w])
                t0 = tf[:, 0, :, 0:w]
                tt(eng, t0, src_lohi, ARb, ALU.mult, Bsrc + [S0], [B_tmpg])
                tt(eng, tf[0:64, 1, :, 0:w], src_lohi[64:128], AIt[64:128, :].unsqueeze(2).to_broadcast([64, 32, w]), ALU.mult, Bsrc + [S0], [B_tmpg])
                tt(eng, tf[64:128, 1, :, 0:w], src_lohi[0:64], AIt[0:64, :].unsqueeze(2).to_broadcast([64, 32, w]), ALU.mult, Bsrc + [S0], [B_tmpg])
                tt(eng, t0, t0, tf[:, 1, :, 0:w], ALU.add, [B_tmpg], [B_tmpg])
                tt(eng, dst, t0, addend, ALU.add, [B_tmpg] + Bsrc, Bdst)

            r4 = E.rearrange("p a b c -> p (a b c)")[:, 0:1024].rearrange("p (g c) -> p g c", g=32)
            B_r4 = B_Ek[0]

            r8 = E.rearrange("p a b c -> p (a b c)")[:, 1024:1536].rearrange("p (g c) -> p g c", g=32)
            r16 = E.rearrange("p a b c -> p (a b c)")[:, 1536:1792].rearrange("p (g c) -> p g c", g=32)

            def scan_gen():
                cp(SE, Xbf[:, :, 0], carry[:], [B_carry], [B_Xbf])
                for ch in range(4):
                    c0 = 32 * ch
                    cmul_A(SE, r2[:, :, 16 * ch:16 * ch + 16], rb[:, :, c0:c0 + 32:2], rb[:, :, c0 + 1:c0 + 32:2], AR, AI, [B_rb], [B_tmpA])
                    yield
                for ch in range(2):
                    c0 = 32 * ch
                    cmul_A(SE, r4[:, :, 16 * ch:16 * ch + 16], r2[:, :, c0:c0 + 32:2], r2[:, :, c0 + 1:c0 + 32:2], AR2, AI2, [B_tmpA], [B_r4])
                    yield
                cmul_A(SE, r8[:, :, 0:16], r4[:, :, 0:32:2], r4[:, :, 1:32:2], AR4, AI4, [B_r4], [B_r4])
                yield
                cmul_A(SE, r16[:, :, 0:8], r8[:, :, 0:16:2], r8[:, :, 1:16:2], AR8, AI8, [B_r4], [B_r4], w=8)
                yield
                rg = ring[0]; Brg = B_ring[0]
                for cc in range(8):
                    if cc == 0:
                        xp = carry[:]; Bxp = B_carry
                    else:
                        xp = rg[:, :, cc - 1]; Bxp = Brg
                    tt(SE, st1, xp, AR16[:], ALU.mult, [Bxp, S0], [B_st1])
                    tt(SE, st2[0:64, :], xp[64:128, :], AI16[64:128, :], ALU.mult, [Bxp, S0], [B_st2])
                    tt(SE, st2[64:128, :], xp[0:64, :], AI16[0:64, :], ALU.mult, [Bxp, S0], [B_st2])
                    tt(SE, st1, st1, r16[:, :, cc], ALU.add, [B_st1, B_r4], [B_st1])
                    tt(SE, rg[:, :, cc], st1, st2, ALU.add, [B_st1, B_st2], [Brg])
                    yield
                cp(SE, Xbf[:, :, 16:128:16], rg[:, :, 0:7], [Brg], [B_Xbf])
                cp(SE, carry[:], rg[:, :, 7], [Brg], [B_carry])
                cmul_A("pool", Xbf[:, :, 8:128:16], Xbf[:, :, 0:128:16], r8[:, :, 0:16:2], AR8, AI8, [B_Xbf, B_r4], [B_Xbf], w=8)
                yield
                cmul_A("pool", Xbf[:, :, 4:128:8], Xbf[:, :, 0:128:8], r4[:, :, 0:32:2], AR4, AI4, [B_Xbf, B_r4], [B_Xbf])
                yield
                for e0 in (0, 64):
                    cmul_A("pool", Xbf[:, :, e0 + 2:e0 + 64:4], Xbf[:, :, e0:e0 + 64:4], r2[:, :, e0 // 2:e0 // 2 + 32:2], AR2, AI2, [B_Xbf, B_tmpA], [B_Xbf])
                    yield
                for e1 in (0, 32, 64, 96):
                    cmul_A("pool", Xbf[:, :, e1 + 1:e1 + 32:2], Xbf[:, :, e1:e1 + 32:2], rb[:, :, e1:e1 + 32:2], AR, AI, [B_Xbf, B_rb], [B_Xbf])
                    yield
            sg = scan_gen()

            def scan_advance(k):
                for _ in range(k):
                    if next(sg, "done") == "done":
                        break
            def zs_proj(t):
                for kt in range(8):
                    mm(psS[:], bigT[:, kt, t::8], win(kt, WZS, 512), kt == 0, kt == 7, [B_bigTh[0], B_bigTh[1], (B_wb0 if kt < 3 else B_wbuf)], [B_psS])
                act(zs[:, t, :], psS[:], AF.Copy, [B_psS], [B_zs])

            def za_proj(qb):
                for kt in range(8):
                    mm(psS[:], bigT[:, kt, qb * 128:(qb + 1) * 128], win(kt, WZA, 512), kt == 0, kt == 7, [B_bigTh[0], B_bigTh[1], (B_wb0 if kt < 3 else B_wbuf)], [B_psS])
                act(sig[qb % 2], psS[:], AF.Tanh, [B_psS], [B_sig[qb % 2]], scale=0.5)
                act(za[:, qb, :], psS[:], AF.Copy, [B_psS], [B_za], scale=0.5)
            for j in range(4):
                for n in range(2):
                    bk = bank()
                    for kt in range(8):
                        mm(psA[:, bk, :], win(kt, WQ + j * 128, 128), bigT[:, kt, n * 512:(n + 1) * 512], kt == 0, kt == 7, [B_bigTh[0], B_bigTh[1], (B_wb0 if kt < 3 else B_wbuf)], [B_psA[bk]])
                    cp("act", QT[:, j, n * 512:(n + 1) * 512], psA[:, bk, :], [B_psA[bk]], [B_QT])
                    scan_advance(1)
            for n in range(2):
                bk = bank()
                for kt in range(8):
                    mm(psA[:, bk, :], win(kt, WK, 128), bigT[:, kt, n * 512:(n + 1) * 512], kt == 0, kt == 7, [B_bigTh[0], B_bigTh[1], (B_wb0 if kt < 3 else B_wbuf)], [B_psA[bk]])
                for kv in range(2):
                    for e in range(2):
                        cp("act", KT[e * 64:(e + 1) * 64, kv * 2 + e, 128 + n * 512:128 + (n + 1) * 512],
                           psA[kv * 64:(kv + 1) * 64, bk, :], [B_psA[bk]], [B_KT])
                scan_advance(2)
            for qh in range(2):
                bk = bank()
                for q4 in range(4):
                    qb = qh * 4 + q4
                    for kt in range(8):
                        mm(psA[:, bk, q4 * 128:(q4 + 1) * 128], bigT[:, kt, qb * 128:(qb + 1) * 128], win(kt, WV, 128), kt == 0, kt == 7, [B_bigTh[0], B_bigTh[1], (B_wb0 if kt < 3 else B_wbuf)], [B_psA[bk]])
                for q4 in range(4):
                    qb = qh * 4 + q4
                    cp("act", V[:, 1 + qb, :, 0:64], psA[:, bk, q4 * 128:(q4 + 1) * 128].rearrange("p (a b) -> p a b", a=2), [B_psA[bk]], [B_V])
                scan_advance(2)
            den8v = den8.rearrange("p (a h) -> p a h", a=2)
            rden8v = rden8.rearrange("p (a h) -> p a h", a=2)
            esinkv = esink[:].rearrange("p (a h) -> p a h", a=2)

            def p4_S(qb, kv):
                for kb in range(2):
                    bk = kv * 2 + kb
                    for e in range(2):
                        for jj in range(2):
                            hl = 2 * jj + e
                            mm(psA[:, bk, hl * 128:(hl + 1) * 128], identb[:], MD[:, kb, 4 * kv + hl, :], True, False, [S0], [B_psA[bk]])
                            mm(psA[:, bk, hl * 128:(hl + 1) * 128],
                               KT[:, kv * 2 + e, (qb + kb) * 128:(qb + kb + 1) * 128],
                               QT[:, 2 * kv + jj, qb * 128:(qb + 1) * 128], False, True, [B_KT, B_QT], [B_psA[bk]])
                    act(PT[:, kb, 4 * kv:4 * kv + 4, :], psA[:, bk, :].rearrange("p (a b) -> p a b", a=4), AF.Exp, [B_psA[bk]], [B_PTk[kv]], scale=0.125)

            def p4_V(qb, kv):
                for hl in range(4):
                    h_ = 4 * kv + hl
                    for kb in range(2):
                        mm(psB[:, kv, hl * 65:hl * 65 + 65], PT[:, kb, h_, :], V[:, qb + kb, kv, :], kb == 0, kb == 1, [B_PTk[kv], B_V], [B_psB[kv]])
                pv = psB[:, kv, 0:260].rearrange("p (h d) -> p h d", h=4)
                tt("dve", den8v[:, kv, :], pv[:, :, 64], esinkv[:, kv, :], ALU.add, [B_psB[kv], S0], [B_denk[kv]])
                fw.op("dve", lambda h, kv=kv: h.reciprocal(out=rden8v[:, kv, :], in_=den8v[:, kv, :]), [B_denk[kv]], [B_denk[kv]])
                tt("dve", ao[:, 4 * kv:4 * kv + 4, :], pv[:, :, 0:64], rden8v[:, kv, :].unsqueeze(2).to_broadcast([128, 4, 64]), ALU.mult, [B_psB[kv], B_denk[kv]], [B_aok[kv]])

            def p4_F(qb):
                stt(za[:, qb, :], sig[qb % 2], 1.0, za[:, qb, :], ALU.add, ALU.mult, [B_sig[qb % 2], B_za], [B_za])
                tt("dve", za[:, qb, :], ao.rearrange("p h d -> p (h d)"), za[:, qb, :], ALU.mult, [B_aok[0], B_aok[1], B_za], [B_za])

            def p4_T(qb):
                pst, Bp = (psT, B_psT) if qb % 2 == 0 else (psS16, B_psS)
                for f in range(4):
                    tr(pst[:, f * 128:(f + 1) * 128], za[:, qb, f * 128:(f + 1) * 128], identb[:], [B_za, S0], [Bp])
                cp("act" if qb % 2 == 0 else "dve", bigT[:, 4:8, qb * 128:(qb + 1) * 128], pst[:, 0:512].rearrange("p (a b) -> p a b", a=4), [Bp], [B_bigTh[1]])
            units = [(qb, kv) for qb in range(8) for kv in range(2)]
            for i in range(len(units) + 2):
                if 0 <= i - 2 < len(units):
                    qb, kv = units[i - 2]
                    p4_V(qb, kv)
                    if kv == 1:
                        p4_F(qb)
                if i < len(units):
                    p4_S(*units[i])
                    if i % 2 == 0:
                        za_proj(i // 2)
                    else:
                        zs_proj(i // 2)
                    scan_advance(2)
            scan_advance(1000)
            act(zs.rearrange("p a b -> p (a b)"), zs.rearrange("p a b -> p (a b)"), AF.Silu, [B_zs], [B_zs])
            for qb in range(8):
                p4_T(qb)
            fw.dma(wbuf[:, :], scr_p5[:, :], d_wb, reads=[B_scr], writes=[B_wbuf, B_wb0])
            scan_advance(1000)
            gv = uSM.rearrange("p (t g c) -> p t g c", t=8, g=32)
            for g4 in range(8):
                pbk, Bp = (psS[:], B_psS) if g4 % 2 == 0 else (psA[:, 3, :], B_psA[3])
                for j in range(4):
                    g = 4 * g4 + j
                    mm(pbk[:, j * 128:(j + 1) * 128], UgT[:, g, :], TT[:, g, :], True, False, [B_UgT, S0], [Bp])
                    mm(pbk[:, j * 128:(j + 1) * 128], Xbf[:, g, :], OTb[:, g, :], False, True, [B_Xbf, S0], [Bp])
                act(gv[:, :, 4 * g4:4 * g4 + 4, :], pbk.rearrange("p (j t c) -> p t j c", j=4, t=8), AF.Gelu_apprx_tanh, [Bp], [B_uSM])
            gv2 = uSM.rearrange("p (t f) -> p t f", t=8)

            def glu_s1(t):
                k = t % 2
                for f in range(4):
                    tr(psT[:, f * 128:(f + 1) * 128], gv2[:, t, f * 128:(f + 1) * 128], identb[:], [B_uSM, S0], [B_psT])
                cp("dve", gTs[k], psT[:, 0:512].rearrange("p (a b) -> p a b", a=4), [B_psT], [B_gTs[k]])

            def glu_s2(t):
                k = t % 2
                for f in range(4):
                    mm(psB[:, k, :], gTs[k][:, f, :], wglu[:, f, :], f == 0, False, [B_gTs[k], B_wglu], [B_psB[k]])
                mm(psB[:, k, :], onesb[:], bglu[:], False, True, [S0], [B_psB[k]])
                act(sig[k], psB[:, k, :], AF.Sigmoid, [B_psB[k]], [B_sig[k]])
                tt("dve", so[k], gv2[:, t, :], zs[:, t, :], ALU.mult, [B_uSM, B_zs], [B_so[k]])
                tt("dve", so[k], so[k], sig[k], ALU.mult, [B_so[k], B_sig[k]], [B_so[k]])

            def glu_s3(t):
                k = t % 2
                pst, Bp = (psS16, B_psS) if k == 0 else (psA3_16, B_psA[3])
                for f in range(4):
                    tr(pst[:, f * 128:(f + 1) * 128], so[k][:, f * 128:(f + 1) * 128], identb[:], [B_so[k], S0], [Bp])
                cp("act" if k == 0 else "dve", bigT[:, 0:4, t * 128:(t + 1) * 128], pst[:, 0:512].rearrange("p (a b) -> p a b", a=4), [Bp], [B_bigTh[0]])
            for t in range(10):
                if t < 8:
                    glu_s1(t)
                if 0 <= t - 1 < 8:
                    glu_s2(t - 1)
                if 0 <= t - 2 < 8:
                    glu_s3(t - 2)
            p5slot = {}

            def p5_A(t):
                xl = xslot[0] % 4; xslot[0] += 1
                sl = t % 2
                p5slot[t] = (sl, xl)
                fw.dma(xs[xl], xv[blk, :, t, :], d_xs[xl], reads=[], writes=[B_xs[xl]])
                fw.dma(pslab[sl], pv[blk, :, t, :], d_ps[sl], writes=[B_ps[sl]])
                for n in range(2):
                    for kt in range(8):
                        mT = bigT[:, kt, t * 128:(t + 1) * 128] if kt < 4 else bigT[:, kt, t::8]
                        mm(psA[:, n, :], mT, wbuf[:, WOUT + kt * 1024 + n * 512:WOUT + kt * 1024 + (n + 1) * 512], kt == 0, kt == 7, [B_bigTh[0], B_bigTh[1]] + ([B_wb0] if kt <= 5 else [B_wb0, B_wbuf]), [B_psA[n]])

            def p5_A2(t):
                sl, xl = p5slot[t]
                act(hn[sl], psA[:, 0:2, :].rearrange("p a b -> p (a b)"), AF.Square, [B_psA[0], B_psA[1]], [B_hn[sl], B_ss2], accum=ss2[:, t:t + 1])
                ts("dve", rstd2[:, t:t + 1], ss2[:, t:t + 1], 1.0 / 1024.0, EPS, ALU.mult, ALU.add, [B_ss2], [B_rstd2])
                tt("pool", rstd2[:, t:t + 1], rstd2[:, t:t + 1], mhalf[:, 0:1], ALU.pow, [B_rstd2, B_mhalf], [B_rstd2])
                stt(tmpA, psA[:, 0:2, :].rearrange("p a b -> p (a b)"), rstd2[:, t:t + 1], gpost[:], ALU.mult, ALU.mult, [B_psA[0], B_psA[1], B_rstd2, B_gpost], [B_tmpA])
                tt("dve", xs[xl], xs[xl], tmpA, ALU.add, [B_xs[xl], B_tmpA], [B_xs[xl]])
                cp("act", hn[sl], xs[xl], [B_xs[xl]], [B_hn[sl]])
                cp("pool", pb[sl], pslab[sl], [B_ps[sl]], [B_pb[sl]])

            def p5_B(t):
                sl, xl = p5slot[t]
                for kt in range(8):
                    tr(psT[:, kt * 128:(kt + 1) * 128], hn[sl][:, kt * 128:(kt + 1) * 128], identb[:], [B_hn[sl], S0], [B_psT])
                cp("dve", hTs, psT[:].rearrange("p (a b) -> p a b", a=8), [B_psT], [B_hTs])

            def p5_B2(t):
                sl, xl = p5slot[t]
                for n in range(2):
                    for kt in range(8):
                        mm(psA[:, 2 + n, :], hTs[:, kt, :], wbuf[:, WGATE + kt * 1024 + n * 512:WGATE + kt * 1024 + (n + 1) * 512], kt == 0, False, [B_hTs, B_wbuf], [B_psA[2 + n]])
                    mm(psA[:, 2 + n, :], onesb[:], wbuf[:, WBG + n * 512:WBG + (n + 1) * 512], False, True, [S0, B_wbuf], [B_psA[2 + n]])
                act(tmpg, psA[:, 2:4, :].rearrange("p a b -> p (a b)"), AF.Sigmoid, [B_psA[2], B_psA[3]], [B_tmpg])
                for kt in range(2):
                    tr(psT[:, kt * 128:(kt + 1) * 128], pb[sl][:, kt * 128:(kt + 1) * 128], identb[:], [B_pb[sl], S0], [B_psT])
                cp("dve", pTs, psT[:, 0:256].rearrange("p (a b) -> p a b", a=2), [B_psT], [B_pTs])
                for n in range(2):
                    for kt in range(2):
                        mm(psB[:, n, :], pTs[:, kt, :], wbuf[:, WPROJ + kt * 1024 + n * 512:WPROJ + kt * 1024 + (n + 1) * 512], kt == 0, kt == 1, [B_pTs, B_wbuf], [B_psB[n]])
                tt("dve", tmpg, tmpg, psB[:, 0:2, :].rearrange("p a b -> p (a b)"), ALU.mult, [B_tmpg, B_psB[0], B_psB[1]], [B_tmpg])
                tt("dve", xs[xl], xs[xl], tmpg, ALU.add, [B_xs[xl], B_tmpg], [B_xs[xl]])
                fw.dma(ov[blk, :, t, :], xs[xl], d_out[xl], reads=[B_xs[xl]])

            p5_A(0)
            p5_A2(0)
            for t in range(8):
                if t + 1 < 8:
                    p5_A(t + 1)
                    if t + 1 == 7 and blk + 1 < NBLK:
                        fw.dma(wbuf[:, 0:6912], scr_in[:, 0:6912], d_wb0, reads=[B_scr], writes=[B_wb0])
                p5_B(t)
                if t + 1 < 8:
                    p5_A2(t + 1)
                if t == 7 and blk + 1 < NBLK:
                    for tt_ in (0, 1):
                        p1_A(blk + 1, tt_)
                        p1_early.add((blk + 1, tt_))
                p5_B2(t)

        if taps:
            d_tap = fw.dsem("tap")
            tapsrc = {"TT": (TT, B_res), "RT": (RT, B_res), "OTb": (OTb, B_res), "AR": (AR, B_res), "AI": (AI, B_res),
                      "MD": (MD, B_res), "bigT": (bigT, B_bigT), "uSM": (uSM, B_uSM), "zs": (zs, B_zs), "za": (za, B_za),
                      "QT": (QT, B_QT), "KT": (KT, B_KT), "Xbf": (Xbf, B_Xbf), "rb": (rb, B_rb), "UgT": (UgT, B_UgT)}
            fw.barrier()
            tstage = arena.carve([128, 4096], F32) if False else None
            for name, shape in taps.items():
                src, bsrc = tapsrc[name]
                n = 1
                for s_ in shape[1:]:
                    n *= s_
                flat = src if len(src.shape) == 2 else (src.rearrange("p a b -> p (a b)") if len(src.shape) == 3 else src.rearrange("p a b c -> p (a b c)"))
                if src.dtype == F32:
                    fw.dma(tap_d[name][:, :], flat, d_tap, reads=[bsrc])
                else:
                    for c0 in range(0, n, 1024):
                        c1 = min(n, c0 + 1024)
                        cp("dve", xs[0][:, 0:c1 - c0], flat[:, c0:c1], [bsrc, B_xs[0]], [B_xs[0]])
                        fw.dma(tap_d[name][:, c0:c1], xs[0][:, 0:c1 - c0], d_tap, reads=[B_xs[0]])
                        fw.barrier()
        fw.barrier()
        fw.emit_all()
    return nc


def _prep_inputs(x, p, pre_norm_g, w_in, ssm_lam_re, ssm_lam_im, ssm_log_step, ssm_b_re, ssm_b_im,
                 ssm_c_re, ssm_c_im, ssm_d, ssm_w_glu, ssm_b_glu, attn_sinks, w_out, post_norm_g,
                 pl_w_proj, pl_w_gate, pl_b_gate):
    f = np.float32

    def kt_layout(w):
        K, N = w.shape
        return np.ascontiguousarray(w.reshape(K // 128, 128, N).transpose(1, 0, 2).reshape(128, (K // 128) * N)).astype(f)
    shared = {}
    shared["w_in_r"] = kt_layout(np.asarray(w_in[0]))
    bg = np.zeros((128, 1024), f)
    bg[0, :] = np.asarray(pl_b_gate[0])
    shared["w5"] = np.ascontiguousarray(np.concatenate(
        [kt_layout(np.asarray(w_out[0])), kt_layout(np.asarray(pl_w_gate[0])), kt_layout(np.asarray(pl_w_proj[0])), bg], axis=1))
    shared["wglu_r"] = kt_layout(np.asarray(ssm_w_glu[0]))
    bgl = np.zeros((128, 512), f)
    bgl[0, :] = np.asarray(ssm_b_glu[0])
    shared["bglu_pad"] = bgl
    shared["gcol"] = np.ascontiguousarray(np.asarray(pre_norm_g[0]).reshape(8, 128).T).astype(f)
    shared["gpost_t"] = np.ascontiguousarray(np.broadcast_to(np.asarray(post_norm_g[0])[None, :], (128, 1024))).astype(f)
    lre = np.asarray(ssm_lam_re[0]).T
    lim = np.asarray(ssm_lam_im[0]).T
    shared["lamre2"] = np.ascontiguousarray(np.concatenate([lre, lre], 0)).astype(f)
    shared["lamim2"] = np.ascontiguousarray(np.concatenate([lim, lim], 0)).astype(f)
    shared["lstep"] = np.ascontiguousarray(np.broadcast_to(np.asarray(ssm_log_step[0])[None, :], (128, 32))).astype(f)
    bre = np.asarray(ssm_b_re[0]).transpose(1, 0, 2).reshape(64, 512)
    bim = np.asarray(ssm_b_im[0]).transpose(1, 0, 2).reshape(64, 512)
    shared["bA"] = np.ascontiguousarray(np.concatenate([bre, bim], 0)).astype(f)
    shared["bB"] = np.ascontiguousarray(np.concatenate([bim, bre], 0)).astype(f)
    cre = np.asarray(ssm_c_re[0]).transpose(2, 0, 1).reshape(64, 512)
    cim = np.asarray(ssm_c_im[0]).transpose(2, 0, 1).reshape(64, 512)
    shared["cA"] = np.ascontiguousarray(np.concatenate([cre, cim], 0)).astype(f)
    shared["cB"] = np.ascontiguousarray(np.concatenate([cim, cre], 0)).astype(f)
    d = np.asarray(ssm_d[0]).reshape(32, 16)
    shared["dcol"] = np.ascontiguousarray(np.tile(d.T, (8, 1))).astype(f)
    shared["sinks_b"] = np.ascontiguousarray(np.broadcast_to(np.asarray(attn_sinks[0])[None, :], (128, 8))).astype(f)
    shared["identf"] = np.eye(128, dtype=f)
    sq_s = np.arange(128) // 16
    shared["mcausal"] = (sq_s[:, None] <= sq_s[None, :]).astype(f)
    s_idx = np.arange(128)[:, None]
    q_idx = np.arange(128)[None, :]
    prev = np.where(s_idx > q_idx, (q_idx + 128 - s_idx).astype(f), f(BIG))
    cur = np.where(s_idx <= q_idx, (q_idx - s_idx).astype(f), f(BIG))
    shared["negtab"] = np.ascontiguousarray(np.concatenate([prev, cur], 1)).astype(f)
    sg = np.ones((128, 1), f)
    sg[0:64] = -1.0
    shared["sgn"] = sg
    xs_ = np.asarray(x)
    ps_ = np.asarray(p[0])
    in_maps = []
    for c in range(NCORES):
        m = dict(shared)
        m["x"] = np.ascontiguousarray(xs_[2 * c:2 * c + 2].reshape(4096, 1024)).astype(f)
        m["p"] = np.ascontiguousarray(ps_[2 * c:2 * c + 2].reshape(4096, 256)).astype(f)
        in_maps.append(m)
    return in_maps


_NC_CACHE = {}


def kernel(**inputs):
    in_maps = _prep_inputs(**inputs)
    if "nc" not in _NC_CACHE:
        _NC_CACHE["nc"] = build_nc()
    nc = _NC_CACHE["nc"]
    res = run_bass_kernel_spmd(nc, in_maps, core_ids=list(range(NCORES)))
    outs = [np.asarray(r["out"]).reshape(2, 2048, 1024) for r in res.results]
    return np.concatenate(outs, axis=0).astype(np.float32)
```

```python
import math
from contextlib import ExitStack

import numpy as np
import concourse.bass as bass
import concourse.mybir as mybir
from concourse.bass_utils import run_bass_kernel_spmd

F32 = mybir.dt.float32
BF16 = mybir.dt.bfloat16
AF = mybir.ActivationFunctionType
ALU = mybir.AluOpType
NCORES = 8
NBLK = 4
PI = math.pi
EPS = 1e-6
BIG = 1.0e6


class Buf:
    def __init__(self, name):
        self.name = name
        self.last_w = None
        self.readers = []


class Eng:
    def __init__(self, name, handle, sem):
        self.name, self.h, self.sem = name, handle, sem
        self.count = 0
        self.seen = {}
        self.prog = []


class FW:
    def __init__(self, nc, es):
        self.nc = nc
        self.es = es
        self.engs = {}
        for name, h in [("pe", nc.tensor), ("act", nc.scalar), ("dve", nc.vector),
                        ("pool", nc.gpsimd), ("sp", nc.sync)]:
            sem = es.enter_context(nc.semaphore("s_" + name))
            self.engs[name] = Eng(name, h, sem)
        self.dsems = []

    def _waits(self, eng, reads, writes):
        deps = {}

        def add(d, raw):
            if d is None:
                return
            key, sem, val = d
            if key == eng.name and eng.name in ("pe", "sp"):
                return
            if key not in deps or deps[key][1] < val:
                deps[key] = (sem, val)
        for b in reads:
            add(b.last_w, True)
        for b in writes:
            add(b.last_w, False)
            for r in b.readers:
                add(r, False)
        out = []
        for key, (sem, val) in deps.items():
            if eng.seen.get(key, 0) >= val:
                continue
            eng.seen[key] = val
            out.append((sem, val))
        return out

    def op(self, engname, fn, reads=(), writes=()):
        eng = self.engs[engname]
        waits = self._waits(eng, reads, writes)
        eng.count += 1
        cnt = eng.count
        sem = eng.sem
        h = eng.h

        def emit():
            for (s, v) in waits:
                h.wait_ge(s, v)
            fn(h).then_inc(sem, 1)
        eng.prog.append(emit)
        d = (eng.name, sem, cnt)
        for b in writes:
            b.last_w = d
            b.readers = []
        for b in reads:
            if b not in writes:
                b.readers.append(d)

    def dsem(self, name):
        d = {"sem": self.es.enter_context(self.nc.semaphore("d_" + name)), "val": 0, "key": "dma_" + name}
        self.dsems.append(d)
        return d

    def dma(self, out, in_, dsem, reads=(), writes=(), engname="sp"):
        eng = self.engs[engname]
        waits = [w for w in self._waits(eng, reads, writes) if w[0] is not dsem["sem"]]
        dsem["val"] += 16
        val = dsem["val"]
        s = dsem["sem"]
        h = eng.h

        def emit():
            for (s2, v) in waits:
                h.wait_ge(s2, v)
            h.dma_start(out=out, in_=in_).then_inc(s, 16)
        eng.prog.append(emit)
        d = (dsem["key"], s, val)
        for b in writes:
            b.last_w = d
            b.readers = []
        for b in reads:
            b.readers.append(d)

    def barrier(self):
        targets = [(e.name, e.sem, e.count) for e in self.engs.values() if e.count > 0]
        targets += [(d["key"], d["sem"], d["val"]) for d in self.dsems if d["val"] > 0]
        for eng in self.engs.values():
            lst = []
            for key, sem, val in targets:
                if key == eng.name:
                    continue
                if eng.seen.get(key, 0) >= val:
                    continue
                eng.seen[key] = val
                lst.append((sem, val))
            h = eng.h

            def emit(lst=lst, h=h):
                for (s, v) in lst:
                    h.wait_ge(s, v)
            eng.prog.append(emit)

    def emit_all(self):
        nc = self.nc
        with nc.Block() as block:
            @block.tensor
            def _(e):
                for f in self.engs["pe"].prog:
                    f()

            @block.scalar
            def _(e):
                for f in self.engs["act"].prog:
                    f()

            @block.vector
            def _(e):
                for f in self.engs["dve"].prog:
                    f()

            @block.gpsimd
            def _(e):
                for f in self.engs["pool"].prog:
                    f()

            @block.sync
            def _(e):
                for f in self.engs["sp"].prog:
                    f()


class Arena:
    def __init__(self, ap_bf16, nbytes):
        self.ap = ap_bf16
        self.nbytes = nbytes
        self.off = 0

    def reset(self):
        self.off = 0

    def carve(self, shape, dt):
        esz = 4 if dt == F32 else 2
        n = 1
        for s in shape[1:]:
            n *= s
        nb = n * esz
        nb_al = (nb + 63) // 64 * 64
        assert self.off + nb_al <= self.nbytes, ("arena overflow", self.off, nb_al, self.nbytes)
        v = self.ap[0:shape[0], self.off // 2:(self.off + nb) // 2]
        self.off += nb_al
        if dt == F32:
            v = v.bitcast(F32)
        if len(shape) == 3:
            v = v.rearrange("p (a b) -> p a b", a=shape[1])
        elif len(shape) == 4:
            v = v.rearrange("p (a b c) -> p a b c", a=shape[1], b=shape[2])
        return v


def build_nc(taps=None):
    nc = bass.Bass("TRN2", target_bir_lowering=False)

    def din(name, shape, dt=F32):
        return nc.dram_tensor(name, shape, dt, kind="ExternalInput").ap()
    x_d = din("x", [4096, 1024])
    p_d = din("p", [4096, 256])
    win_d = din("w_in_r", [128, 18432])
    w5_d = din("w5", [128, 19456])
    wglu_d = din("wglu_r", [128, 2048])
    bglu_d = din("bglu_pad", [128, 512])
    gcol_d = din("gcol", [128, 8])
    gpost_d = din("gpost_t", [128, 1024])
    lamre_d = din("lamre2", [128, 32])
    lamim_d = din("lamim2", [128, 32])
    lstep_d = din("lstep", [128, 32])
    bA_d = din("bA", [128, 512])
    bB_d = din("bB", [128, 512])
    cA_d = din("cA", [128, 512])
    cB_d = din("cB", [128, 512])
    dcol_d = din("dcol", [128, 32])
    sinks_d = din("sinks_b", [128, 8])
    identf_d = din("identf", [128, 128])
    mcausal_d = din("mcausal", [128, 128])
    negtab_d = din("negtab", [128, 256])
    sgn_d = din("sgn", [128, 1])
    out_d = nc.dram_tensor("out", [4096, 1024], F32, kind="ExternalOutput").ap()
    scr_in = nc.dram_tensor("scr_in", [128, 18432], BF16, kind="Internal").ap()
    scr_p5 = nc.dram_tensor("scr_p5", [128, 19456], BF16, kind="Internal").ap()
    tap_d = {}
    if taps:
        for name, shape in taps.items():
            tap_d[name] = nc.dram_tensor("tap_" + name, list(shape), F32, kind="ExternalOutput").ap()

    xv = x_d.rearrange("(b i t) d -> b i t d", b=NBLK, i=128, t=8)
    pv = p_d.rearrange("(b i t) d -> b i t d", b=NBLK, i=128, t=8)
    ov = out_d.rearrange("(b i t) d -> b i t d", b=NBLK, i=128, t=8)

    with ExitStack() as es:
        fw = FW(nc, es)

        def sb(name, shape, dt):
            return es.enter_context(nc.sbuf_tensor(name, shape, dt))

        wbuf = sb("wbuf", [128, 19456], BF16)
        wglu = sb("wglu", [128, 4, 512], BF16)
        bglu = sb("bglu", [128, 512], BF16)
        RT = sb("RT", [128, 32, 128], BF16)
        TT = sb("TT", [128, 32, 128], BF16)
        OTb = sb("OTb", [128, 32, 128], BF16)
        gpost = sb("gpost", [128, 1024], F32)
        MD = sb("MD", [128, 2, 8, 128], BF16)
        identb = sb("identb", [128, 128], BF16)
        onesb = sb("onesb", [128, 128], BF16)
        esink = sb("esink", [128, 8], F32)
        AR = sb("AR", [128, 32], F32)
        AI = sb("AI", [128, 32], F32)
        AR2 = sb("AR2", [128, 32], F32)
        AI2 = sb("AI2", [128, 32], F32)
        AR4 = sb("AR4", [128, 32], F32)
        AI4 = sb("AI4", [128, 32], F32)
        AR8 = sb("AR8", [128, 32], F32)
        AI8 = sb("AI8", [128, 32], F32)
        AR16 = sb("AR16", [128, 32], F32)
        AI16 = sb("AI16", [128, 32], F32)
        carry = sb("carry", [128, 32], F32)
        ARENA_BYTES = 130 * 1024
        arena_t = sb("arena", [128, ARENA_BYTES // 2], BF16)
        arena = Arena(arena_t, ARENA_BYTES)
        psA = es.enter_context(nc.psum_tensor("psA", [128, 4, 512], F32))
        psB = es.enter_context(nc.psum_tensor("psB", [128, 2, 512], F32))
        psT = es.enter_context(nc.psum_tensor("psT", [128, 1024], BF16))
        psS = es.enter_context(nc.psum_tensor("psS", [128, 512], F32))
        B_psA = [Buf("psA%d" % i) for i in range(4)]
        B_psB = [Buf("psB%d" % i) for i in range(2)]
        B_psT = Buf("psT")
        psS16 = psS[:].bitcast(BF16)
        psA3_16 = psA[:, 3, :].bitcast(BF16)
        B_psS = Buf("psS")

        def tt(eng, out, in0, in1, op, reads, writes):
            fw.op(eng, lambda h: h.tensor_tensor(out=out, in0=in0, in1=in1, op=op), reads, writes)

        def ts(eng, out, in0, s1, s2, op0, op1, reads, writes):
            if op1 is None:
                fw.op(eng, lambda h: h.tensor_scalar(out=out, in0=in0, scalar1=s1, scalar2=None, op0=op0), reads, writes)
            else:
                fw.op(eng, lambda h: h.tensor_scalar(out=out, in0=in0, scalar1=s1, scalar2=s2, op0=op0, op1=op1), reads, writes)

        def stt(out, in0, scalar, in1, op0, op1, reads, writes):
            fw.op("dve", lambda h: h.scalar_tensor_tensor(out=out, in0=in0, scalar=scalar, in1=in1, op0=op0, op1=op1), reads, writes)

        def act(out, in_, func, reads, writes, scale=None, bias=None, accum=None):
            kw = {}
            if scale is not None:
                kw["scale"] = scale
            if bias is not None:
                kw["bias"] = bias
            if accum is not None:
                kw["accum_out"] = accum
            fw.op("act", lambda h: h.activation(out=out, in_=in_, func=func, **kw), reads, writes)

        def cp(eng, out, in_, reads, writes):
            if eng == "act":
                act(out, in_, AF.Copy, reads, writes)
            else:
                fw.op(eng, lambda h: h.tensor_copy(out=out, in_=in_), reads, writes)

        def mm(out, lhsT, rhs, start, stop, reads, writes):
            fw.op("pe", lambda h: h.matmul(out, lhsT=lhsT, rhs=rhs, start=start, stop=stop), reads, writes)

        def tr(out, in_, ident, reads, writes):
            fw.op("pe", lambda h: h.transpose(out=out, in_=in_, identity=ident), reads, writes)

        def memset(eng, ap, val, writes):
            fw.op(eng, lambda h: h.memset(ap, val), (), writes)

        d_s0 = fw.dsem("s0in")

        def load(dst, src, name, buf):
            fw.dma(dst, src, d_s0, writes=[buf])

        S0 = Buf("S0")
        S0in = Buf("S0in")
        B_res = Buf("res")
        identf = arena.carve([128, 128], F32)
        mcausal = arena.carve([128, 128], F32)
        negtab = arena.carve([128, 2, 128], F32)
        sgn = arena.carve([128, 1], F32)
        nsgn = arena.carve([128, 1], F32)
        lamre = arena.carve([128, 32], F32)
        lamim = arena.carve([128, 32], F32)
        lstep = arena.carve([128, 32], F32)
        bA = arena.carve([128, 32, 16], F32)
        bB = arena.carve([128, 32, 16], F32)
        cA = arena.carve([128, 32, 16], F32)
        cB = arena.carve([128, 32, 16], F32)
        dcol = arena.carve([128, 32], F32)
        sinks = arena.carve([128, 8], F32)
        gcol = arena.carve([128, 8], F32)
        bglu_f = arena.carve([128, 512], F32)
        for dst, src, nm in [(identf, identf_d, "identf"), (mcausal, mcausal_d, "mc"),
                             (negtab, negtab_d.rearrange("p (a b) -> p a b", a=2), "negtab"), (sgn, sgn_d, "sgn"),
                             (lamre, lamre_d, "lamre"), (lamim, lamim_d, "lamim"), (lstep, lstep_d, "lstep"),
                             (bA, bA_d.rearrange("p (a b) -> p a b", a=32), "bA"),
                             (bB, bB_d.rearrange("p (a b) -> p a b", a=32), "bB"),
                             (cA, cA_d.rearrange("p (a b) -> p a b", a=32), "cA"),
                             (cB, cB_d.rearrange("p (a b) -> p a b", a=32), "cB"),
                             (dcol, dcol_d, "dcol"), (sinks, sinks_d, "sinks"), (gcol, gcol_d, "gcol"),
                             (bglu_f, bglu_d, "bgluf")]:
            load(dst, src, nm, S0in)
        B_gpost = Buf("gpost")
        fw.dma(gpost[:], gpost_d, fw.dsem("gpost"), writes=[B_gpost])

        def s0(eng, fn):
            fw.op(eng, fn, [S0, S0in], [S0])

        def T32():
            return arena.carve([128, 32], F32)

        s0("dve", lambda h: h.tensor_copy(out=identb[:], in_=identf))
        s0("dve", lambda h: h.memset(onesb[:], 1.0))
        s0("dve", lambda h: h.tensor_copy(out=bglu[:], in_=bglu_f))
        s0("dve", lambda h: h.tensor_scalar(out=nsgn, in0=sgn, scalar1=-1.0, scalar2=None, op0=ALU.mult))
        s0("act", lambda h: h.activation(out=esink[:], in_=sinks, func=AF.Exp))
        for hh in range(8):
            slope = 2.0 ** (-(hh + 1))
            s0("act", lambda h, hh=hh, slope=slope: h.activation(out=MD[:, :, hh, :], in_=negtab, func=AF.Copy, scale=-8.0 * slope))
        step = T32(); a_ = T32(); th = T32(); mag = T32(); w_ = T32(); rs = T32(); sin1 = T32(); cos1 = T32(); thc = T32()
        s0("act", lambda h: h.activation(out=step, in_=lstep, func=AF.Exp))
        s0("dve", lambda h: h.tensor_tensor(out=a_, in0=lamre, in1=step, op=ALU.mult))
        s0("dve", lambda h: h.tensor_tensor(out=th, in0=lamim, in1=step, op=ALU.mult))
        s0("act", lambda h: h.activation(out=mag, in_=a_, func=AF.Exp))

        def wrap(dst, src):
            first = True
            for m in (1.0, 3.0, 5.0, 7.0):
                s0("dve", lambda h, m=m: h.tensor_scalar(out=w_, in0=src, scalar1=m * PI, scalar2=-2.0 * PI, op0=ALU.is_gt, op1=ALU.mult))
                if first:
                    s0("dve", lambda h: h.tensor_tensor(out=dst, in0=src, in1=w_, op=ALU.add))
                    first = False
                else:
                    s0("dve", lambda h: h.tensor_tensor(out=dst, in0=dst, in1=w_, op=ALU.add))
        wrap(rs, th)
        s0("act", lambda h: h.activation(out=sin1, in_=rs, func=AF.Sin))
        s0("dve", lambda h: h.tensor_scalar(out=thc, in0=th, scalar1=PI / 2.0, scalar2=None, op0=ALU.add))
        wrap(rs, thc)
        s0("act", lambda h: h.activation(out=cos1, in_=rs, func=AF.Sin))
        B_wbuf = Buf("wbuf")
        B_wb0 = Buf("wbuf0")
        B_stg = [Buf("stg0"), Buf("stg1"), Buf("stg2")]
        B_stgb = [Buf("stgb0"), Buf("stgb1")]
        stg = [arena.carve([128, 2304], F32), arena.carve([128, 2304], F32), arena.carve([128, 2304], F32)]
        stgb = [arena.carve([128, 2048], BF16), arena.carve([128, 2048], BF16)]
        d_stg = [fw.dsem("stg0"), fw.dsem("stg1"), fw.dsem("stg2")]
        d_scr = fw.dsem("scr")
        B_scr = Buf("scr")
        cast_engs = ["act", "act", "act", "act"]
        ci = 0
        for kt in range(8):
            sl = kt % 3
            fw.dma(stg[sl], win_d[:, kt * 2304:(kt + 1) * 2304], d_stg[sl], writes=[B_stg[sl]])
            eng = cast_engs[ci % 4]; ci += 1
            dst = wbuf[:, kt * 2304:(kt + 1) * 2304]
            if eng == "act":
                fw.op("act", lambda h, dst=dst, sl=sl, kt=kt: h.activation(out=dst, in_=stg[sl], func=AF.Copy, scale=gcol[:, kt:kt + 1]), [B_stg[sl], S0in], [B_wbuf, B_wb0])
            else:
                fw.op(eng, lambda h, dst=dst, sl=sl, kt=kt: h.tensor_scalar(out=dst, in0=stg[sl], scalar1=gcol[:, kt:kt + 1], scalar2=None, op0=ALU.mult), [B_stg[sl], S0in], [B_wbuf, B_wb0])
        d_stgb = [fw.dsem("stgb0"), fw.dsem("stgb1")]
        nch = (19456 + 2047) // 2048
        for c in range(nch):
            sl = (c + 2) % 3
            sb2 = c % 2
            c0 = c * 2048
            c1 = min(19456, c0 + 2048)
            n = c1 - c0
            fw.dma(stg[sl][:, 0:n], w5_d[:, c0:c1], d_stg[sl], writes=[B_stg[sl]])
            eng = cast_engs[ci % 4]; ci += 1
            cp(eng, stgb[sb2][:, 0:n], stg[sl][:, 0:n], [B_stg[sl]], [B_stgb[sb2]])
            fw.dma(scr_p5[:, c0:c1], stgb[sb2][:, 0:n], d_stgb[sb2], reads=[B_stgb[sb2]])
        sl = (nch + 2) % 3
        fw.dma(stg[sl][:, 0:2048], wglu_d[:, :], d_stg[sl], writes=[B_stg[sl]])
        B_wglu = Buf("wglu")
        cp("act", wglu[:].rearrange("p a b -> p (a b)"), stg[sl][:, 0:2048], [B_stg[sl]], [B_wglu])

        Pre = arena.carve([128, 32, 9], F32)
        Pim = arena.carve([128, 32, 9], F32)
        s0("dve", lambda h: h.memset(Pre[:, :, 0:1], 1.0))
        s0("dve", lambda h: h.memset(Pim[:, :, 0:1], 0.0))
        s0("dve", lambda h: h.tensor_tensor(out=Pre[:, :, 1], in0=mag, in1=cos1, op=ALU.mult))
        s0("dve", lambda h: h.tensor_tensor(out=Pim[:, :, 1], in0=mag, in1=sin1, op=ALU.mult))
        ta = arena.carve([128, 32, 8], F32)
        tb = arena.carve([128, 32, 8], F32)

        def cmul(ore, oim, are, aim, bre, bim, shape):
            k = shape[2]
            bre_b = bre.to_broadcast(shape) if bre.shape[2] == 1 and k > 1 else bre
            bim_b = bim.to_broadcast(shape) if bim.shape[2] == 1 and k > 1 else bim
            s0("dve", lambda h: h.tensor_tensor(out=ta[:, :, 0:k], in0=are, in1=bre_b, op=ALU.mult))
            s0("dve", lambda h: h.tensor_tensor(out=tb[:, :, 0:k], in0=aim, in1=bim_b, op=ALU.mult))
            s0("dve", lambda h: h.tensor_tensor(out=ore, in0=ta[:, :, 0:k], in1=tb[:, :, 0:k], op=ALU.subtract))
            s0("dve", lambda h: h.tensor_tensor(out=ta[:, :, 0:k], in0=are, in1=bim_b, op=ALU.mult))
            s0("dve", lambda h: h.tensor_tensor(out=tb[:, :, 0:k], in0=aim, in1=bre_b, op=ALU.mult))
            s0("dve", lambda h: h.tensor_tensor(out=oim, in0=ta[:, :, 0:k], in1=tb[:, :, 0:k], op=ALU.add))
        for k in (1, 2, 4):
            cmul(Pre[:, :, k + 1:2 * k + 1], Pim[:, :, k + 1:2 * k + 1], Pre[:, :, 1:k + 1], Pim[:, :, 1:k + 1],
                 Pre[:, :, k:k + 1], Pim[:, :, k:k + 1], [128, 32, k])
        m2 = arena.carve([128, 32, 1], F32); inv = arena.carve([128, 32, 1], F32)
        pm8re = arena.carve([128, 32, 1], F32); pm8im = arena.carve([128, 32, 1], F32)
        t1a = arena.carve([128, 32, 1], F32)
        P8re = Pre[:, :, 8:9]; P8im = Pim[:, :, 8:9]
        s0("dve", lambda h: h.tensor_tensor(out=m2, in0=P8re, in1=P8re, op=ALU.mult))
        s0("dve", lambda h: h.tensor_tensor(out=t1a, in0=P8im, in1=P8im, op=ALU.mult))
        s0("dve", lambda h: h.tensor_tensor(out=m2, in0=m2, in1=t1a, op=ALU.add))
        s0("dve", lambda h: h.reciprocal(out=inv, in_=m2))
        s0("dve", lambda h: h.tensor_tensor(out=pm8re, in0=P8re, in1=inv, op=ALU.mult))
        s0("dve", lambda h: h.tensor_tensor(out=pm8im, in0=P8im, in1=inv, op=ALU.mult))
        s0("dve", lambda h: h.tensor_scalar(out=pm8im, in0=pm8im, scalar1=-1.0, scalar2=None, op0=ALU.mult))
        nre = arena.carve([128, 32, 1], F32); den = arena.carve([128, 32, 1], F32)
        cre = arena.carve([128, 32, 1], F32); cim = arena.carve([128, 32, 1], F32)
        lre3 = lamre.unsqueeze(2); lim3 = lamim.unsqueeze(2)
        P1re = Pre[:, :, 1:2]; P1im = Pim[:, :, 1:2]
        s0("dve", lambda h: h.tensor_scalar(out=nre, in0=P1re, scalar1=-1.0, scalar2=None, op0=ALU.add))
        s0("dve", lambda h: h.tensor_tensor(out=den, in0=lre3, in1=lre3, op=ALU.mult))
        s0("dve", lambda h: h.tensor_tensor(out=t1a, in0=lim3, in1=lim3, op=ALU.mult))
        s0("dve", lambda h: h.tensor_tensor(out=den, in0=den, in1=t1a, op=ALU.add))
        s0("dve", lambda h: h.reciprocal(out=inv, in_=den))
        s0("dve", lambda h: h.tensor_tensor(out=cre, in0=nre, in1=lre3, op=ALU.mult))
        s0("dve", lambda h: h.tensor_tensor(out=t1a, in0=P1im, in1=lim3, op=ALU.mult))
        s0("dve", lambda h: h.tensor_tensor(out=cre, in0=cre, in1=t1a, op=ALU.add))
        s0("dve", lambda h: h.tensor_tensor(out=cre, in0=cre, in1=inv, op=ALU.mult))
        s0("dve", lambda h: h.tensor_tensor(out=cim, in0=P1im, in1=lre3, op=ALU.mult))
        s0("dve", lambda h: h.tensor_tensor(out=t1a, in0=nre, in1=lim3, op=ALU.mult))
        s0("dve", lambda h: h.tensor_tensor(out=cim, in0=cim, in1=t1a, op=ALU.subtract))
        s0("dve", lambda h: h.tensor_tensor(out=cim, in0=cim, in1=inv, op=ALU.mult))
        Fre = arena.carve([128, 32, 8], F32); Fim = arena.carve([128, 32, 8], F32); F2 = arena.carve([128, 32, 8], F32)
        Gre = arena.carve([128, 32, 8], F32); Gim = arena.carve([128, 32, 8], F32); G2 = arena.carve([128, 32, 8], F32)
        cmul(Fre, Fim, Pre[:, :, 0:8], Pim[:, :, 0:8], cre, cim, [128, 32, 8])
        cmul(Gre, Gim, Fre, Fim, pm8re, pm8im, [128, 32, 8])
        s0("dve", lambda h: h.tensor_scalar(out=F2, in0=Fim, scalar1=sgn, scalar2=None, op0=ALU.mult))
        s0("dve", lambda h: h.tensor_scalar(out=G2, in0=Gim, scalar1=sgn, scalar2=None, op0=ALU.mult))
        p16re = arena.carve([128, 32, 1], F32); p16im = arena.carve([128, 32, 1], F32)
        s0("dve", lambda h: h.tensor_tensor(out=p16re, in0=P8re, in1=P8re, op=ALU.mult))
        s0("dve", lambda h: h.tensor_tensor(out=t1a, in0=P8im, in1=P8im, op=ALU.mult))
        s0("dve", lambda h: h.tensor_tensor(out=p16re, in0=p16re, in1=t1a, op=ALU.subtract))
        s0("dve", lambda h: h.tensor_tensor(out=p16im, in0=P8re, in1=P8im, op=ALU.mult))
        s0("dve", lambda h: h.tensor_scalar(out=p16im, in0=p16im, scalar1=2.0, scalar2=None, op0=ALU.mult))
        p32re = arena.carve([128, 32, 1], F32); p32im = arena.carve([128, 32, 1], F32)
        s0("dve", lambda h: h.tensor_tensor(out=p32re, in0=p16re, in1=p16re, op=ALU.mult))
        s0("dve", lambda h: h.tensor_tensor(out=t1a, in0=p16im, in1=p16im, op=ALU.mult))
        s0("dve", lambda h: h.tensor_tensor(out=p32re, in0=p32re, in1=t1a, op=ALU.subtract))
        s0("dve", lambda h: h.tensor_tensor(out=p32im, in0=p16re, in1=p16im, op=ALU.mult))
        s0("dve", lambda h: h.tensor_scalar(out=p32im, in0=p32im, scalar1=2.0, scalar2=None, op0=ALU.mult))
        p64re = arena.carve([128, 32, 1], F32); p64im = arena.carve([128, 32, 1], F32)
        s0("dve", lambda h: h.tensor_tensor(out=p64re, in0=p32re, in1=p32re, op=ALU.mult))
        s0("dve", lambda h: h.tensor_tensor(out=t1a, in0=p32im, in1=p32im, op=ALU.mult))
        s0("dve", lambda h: h.tensor_tensor(out=p64re, in0=p64re, in1=t1a, op=ALU.subtract))
        s0("dve", lambda h: h.tensor_tensor(out=p64im, in0=p32re, in1=p32im, op=ALU.mult))
        s0("dve", lambda h: h.tensor_scalar(out=p64im, in0=p64im, scalar1=2.0, scalar2=None, op0=ALU.mult))
        p128re = arena.carve([128, 32, 1], F32); p128im = arena.carve([128, 32, 1], F32)
        s0("dve", lambda h: h.tensor_tensor(out=p128re, in0=p64re, in1=p64re, op=ALU.mult))
        s0("dve", lambda h: h.tensor_tensor(out=t1a, in0=p64im, in1=p64im, op=ALU.mult))
        s0("dve", lambda h: h.tensor_tensor(out=p128re, in0=p128re, in1=t1a, op=ALU.subtract))
        s0("dve", lambda h: h.tensor_tensor(out=p128im, in0=p64re, in1=p64im, op=ALU.mult))
        s0("dve", lambda h: h.tensor_scalar(out=p128im, in0=p128im, scalar1=2.0, scalar2=None, op0=ALU.mult))
        s0("dve", lambda h: h.tensor_copy(out=AR16[:], in_=p128re[:, :, 0]))
        s0("dve", lambda h: h.tensor_scalar(out=AI16[:], in0=p128im[:, :, 0], scalar1=nsgn, scalar2=None, op0=ALU.mult))
        s0("dve", lambda h: h.tensor_copy(out=AR8[:], in_=p64re[:, :, 0]))
        s0("dve", lambda h: h.tensor_scalar(out=AI8[:], in0=p64im[:, :, 0], scalar1=nsgn, scalar2=None, op0=ALU.mult))
        s0("dve", lambda h: h.tensor_copy(out=AR4[:], in_=p32re[:, :, 0]))
        s0("dve", lambda h: h.tensor_scalar(out=AI4[:], in0=p32im[:, :, 0], scalar1=nsgn, scalar2=None, op0=ALU.mult))
        s0("dve", lambda h: h.tensor_copy(out=AR2[:], in_=p16re[:, :, 0]))
        s0("dve", lambda h: h.tensor_scalar(out=AI2[:], in0=p16im[:, :, 0], scalar1=nsgn, scalar2=None, op0=ALU.mult))
        s0("dve", lambda h: h.tensor_copy(out=AR[:], in_=Pre[:, :, 8]))
        s0("dve", lambda h: h.tensor_scalar(out=AI[:], in0=Pim[:, :, 8], scalar1=nsgn, scalar2=None, op0=ALU.mult))
        R = arena.carve([128, 32, 8, 16], F32)
        R2 = arena.carve([128, 32, 8, 16], F32)
        OT = arena.carve([128, 32, 8, 16], F32)
        t512a = arena.carve([128, 32, 16], F32)
        t512b = arena.carve([128, 32, 16], F32)
        S0R = Buf("S0R"); S0T = Buf("S0T"); S0P = Buf("S0P"); B_tmp4 = Buf("tmp4"); B_TTb = Buf("TTb"); B_RTb = Buf("RTb")
        t512c = arena.carve([128, 32, 16], F32)
        t512d = arena.carve([128, 32, 16], F32)
        for s_ in range(8):
            k = 7 - s_
            fw.op("dve", lambda h, k=k: h.tensor_tensor(out=t512a, in0=bA, in1=Fre[:, :, k:k + 1].to_broadcast([128, 32, 16]), op=ALU.mult), [S0, S0in, S0T], [S0T])
            fw.op("dve", lambda h, k=k: h.tensor_tensor(out=t512b, in0=bB, in1=F2[:, :, k:k + 1].to_broadcast([128, 32, 16]), op=ALU.mult), [S0, S0in, S0T], [S0T])
            fw.op("dve", lambda h, s_=s_: h.tensor_tensor(out=R[:, :, s_, :], in0=t512a, in1=t512b, op=ALU.add), [S0T], [S0R])
        G1 = arena.carve([128, 32, 8], F32); G2o = arena.carve([128, 32, 8], F32)
        S0O = Buf("S0O")
        fw.op("dve", lambda h: h.tensor_scalar(out=G1, in0=Pre[:, :, 1:9], scalar1=nsgn, scalar2=None, op0=ALU.mult), [S0, S0in], [S0O])
        fw.op("dve", lambda h: h.tensor_scalar(out=G2o, in0=Pim[:, :, 1:9], scalar1=-1.0, scalar2=None, op0=ALU.mult), [S0, S0in, S0O], [S0O])
        for s_ in range(8):
            k = 7 - s_
            fw.op("pool", lambda h, k=k: h.tensor_tensor(out=t512c, in0=bA, in1=Gre[:, :, k:k + 1].to_broadcast([128, 32, 16]), op=ALU.mult), [S0, S0in, S0P], [S0P])
            fw.op("pool", lambda h, k=k: h.tensor_tensor(out=t512d, in0=bB, in1=G2[:, :, k:k + 1].to_broadcast([128, 32, 16]), op=ALU.mult), [S0, S0in, S0P], [S0P])
            fw.op("pool", lambda h, s_=s_: h.tensor_tensor(out=R2[:, :, s_, :], in0=t512c, in1=t512d, op=ALU.add), [S0P], [S0P])
        for t in range(8):
            fw.op("dve", lambda h, t=t: h.tensor_tensor(out=t512a, in0=cA, in1=G1[:, :, t:t + 1].to_broadcast([128, 32, 16]), op=ALU.mult), [S0in, S0O, S0T], [S0T])
            fw.op("dve", lambda h, t=t: h.tensor_tensor(out=t512b, in0=cB, in1=G2o[:, :, t:t + 1].to_broadcast([128, 32, 16]), op=ALU.mult), [S0in, S0O, S0T], [S0T])
            fw.op("dve", lambda h, t=t: h.tensor_tensor(out=OT[:, :, t, :], in0=t512a, in1=t512b, op=ALU.add), [S0T], [S0O])
        Rv = R.rearrange("p g s q -> p g (s q)")
        R2v = R2.rearrange("p g s q -> p g (s q)")
        OTv = OT.rearrange("p g s q -> p g (s q)")
        fw.op("act", lambda h: h.activation(out=OTb[:], in_=OTv, func=AF.Copy), [S0O], [B_TTb])
        tmp4 = [arena.carve([128, 4, 128], F32), arena.carve([128, 4, 128], F32)]
        B_tmp4s = [B_tmp4, Buf("tmp4b")]
        for g4 in range(8):
            pbk, Bp = (psS[:], B_psS) if g4 % 2 == 0 else (psA[:, 0, :], B_psA[0])
            tm, Btm = tmp4[g4 % 2], B_tmp4s[g4 % 2]
            for j in range(4):
                g = 4 * g4 + j
                fw.op("pe", lambda h, g=g, j=j, pbk=pbk: h.matmul(pbk[:, j * 128:(j + 1) * 128], lhsT=R2v[:, g, :], rhs=OTv[:, g, :], start=True, stop=True), [S0P, S0O], [Bp])
            fw.op("dve", lambda h, pbk=pbk, tm=tm: h.tensor_tensor(out=tm, in0=pbk.rearrange("p (a b) -> p a b", a=4), in1=mcausal.unsqueeze(1).to_broadcast([128, 4, 128]), op=ALU.mult), [Bp, S0in], [Btm])
            for j in range(4):
                g = 4 * g4 + j
                fw.op("dve", lambda h, g=g, j=j, tm=tm: h.scalar_tensor_tensor(out=TT[:, g, :], in0=identf, scalar=dcol[:, g:g + 1], in1=tm[:, j, :], op0=ALU.mult, op1=ALU.add), [Btm, S0in], [B_TTb])
        Rv = R.rearrange("p g s q -> p g (s q)")
        for g4 in range(8):
            pbk, Bp = (psB[:, 0, :], B_psB[0]) if g4 % 2 == 0 else (psB[:, 1, :], B_psB[1])
            for j in range(4):
                g = 4 * g4 + j
                fw.op("pe", lambda h, g=g, j=j, pbk=pbk: h.matmul(pbk[:, j * 128:(j + 1) * 128], lhsT=Rv[:, g, :], rhs=identf, start=True, stop=True), [S0R, S0in], [Bp])
            fw.op("act", lambda h, g4=g4, pbk=pbk: h.activation(out=RT[:, 4 * g4:4 * g4 + 4, :], in_=pbk.rearrange("p (a b) -> p a b", a=4), func=AF.Copy), [Bp], [B_RTb])

        fw.barrier()

        arena.reset()
        bigT = arena.carve([128, 8, 1024], BF16)
        uSM = arena.carve([128, 4096], BF16)
        zs = arena.carve([128, 8, 512], BF16)
        za = arena.carve([128, 8, 512], BF16)
        QT = arena.carve([128, 4, 1024], BF16)
        KT = arena.carve([128, 4, 1152], BF16)
        V = arena.carve([128, 9, 2, 65], BF16)
        UgT = arena.carve([128, 32, 128], BF16)
        rb = arena.carve([128, 32, 128], BF16)
        Xbf = arena.carve([128, 32, 128], BF16)
        ring = [arena.carve([128, 32, 16], F32), arena.carve([128, 32, 16], F32)]
        st1 = arena.carve([128, 32], F32)
        st2 = arena.carve([128, 32], F32)
        E = arena.carve([128, 2, 8, 128], BF16)
        PT = arena.carve([128, 2, 8, 128], BF16)
        xs = [arena.carve([128, 1024], F32), arena.carve([128, 1024], F32), E.rearrange("p a b c -> p (a b c)").bitcast(F32),
              Xbf.rearrange("p a b -> p (a b)")[:, 0:2048].bitcast(F32)]
        hn = [arena.carve([128, 1024], BF16), arena.carve([128, 1024], BF16)]
        tmpg = arena.carve([128, 1024], F32)
        hTs = arena.carve([128, 8, 128], BF16)
        pslab = [arena.carve([128, 256], F32), arena.carve([128, 256], F32)]
        pb = [arena.carve([128, 256], BF16), arena.carve([128, 256], BF16)]
        tmpA = arena.carve([128, 1024], F32)
        pTs = arena.carve([128, 2, 128], BF16)
        gTs = [arena.carve([128, 4, 128], BF16), arena.carve([128, 4, 128], BF16)]
        sig = [arena.carve([128, 512], BF16), arena.carve([128, 512], BF16)]
        so = [arena.carve([128, 512], BF16), arena.carve([128, 512], BF16)]
        ao = arena.carve([128, 8, 64], BF16)
        ao2 = arena.carve([128, 512], BF16)
        den8 = arena.carve([128, 8], F32)
        rden8 = arena.carve([128, 8], F32)
        ss = arena.carve([128, 8], F32)
        rstd = arena.carve([128, 8], F32)
        ss2 = arena.carve([128, 8], F32)
        rstd2 = arena.carve([128, 8], F32)

        B_bigT = Buf("bigT"); B_bigTh = [Buf("bigT_lo"), Buf("bigT_hi")]; B_uSM = Buf("uSM"); B_zs = Buf("zs"); B_za = Buf("za"); B_QT = Buf("QT")
        B_KT = Buf("KT"); B_V = Buf("V"); B_UgT = Buf("UgT"); B_rb = Buf("rb"); B_Xbf = Buf("Xbf")
        B_ring = [Buf("ring0"), Buf("ring1")]; B_st1 = Buf("st1"); B_st2 = Buf("st2"); B_carry = Buf("carry")
        B_Ek = [Buf("E0"), Buf("E1")]; B_PTk = [Buf("PT0"), Buf("PT1")]; B_denk = [Buf("den0"), Buf("den1")]; B_aok = [Buf("ao0"), Buf("ao1")]; B_xs = [Buf("xs0"), Buf("xs1"), B_Ek[0], B_Xbf]; B_hn = [Buf("hn0"), Buf("hn1")]
        B_junk = Buf("junk"); B_tmpg = Buf("tmpg"); B_hTs = Buf("hTs"); B_ps = [Buf("pslab0"), Buf("pslab1")]
        B_pb = [Buf("pb0"), Buf("pb1")]; B_tmpA = Buf("tmpA"); B_pTs = Buf("pTs"); B_gTs = [Buf("gTs0"), Buf("gTs1")]; B_sig = [Buf("sig0"), Buf("sig1")]; B_so = [Buf("so0"), Buf("so1")]
        B_ao = Buf("ao"); B_ao2 = Buf("ao2"); B_den = Buf("den"); B_ss = Buf("ss"); B_rstd = Buf("rstd")
        B_ss2 = Buf("ss2"); B_rstd2 = Buf("rstd2")
        d_xs = [fw.dsem("xs0"), fw.dsem("xs1"), fw.dsem("xs2"), fw.dsem("xs3")]
        d_ps = [fw.dsem("ps0"), fw.dsem("ps1")]
        d_out = [fw.dsem("out0"), fw.dsem("out1"), fw.dsem("out2"), fw.dsem("out3")]
        d_wb = fw.dsem("wb")
        d_wb0 = fw.dsem("wb0")

        mhalf = arena.carve([128, 8], F32)
        B_mhalf = Buf("mhalf")
        memset("pool", mhalf, -0.5, [B_mhalf])
        memset("pool", KT, 0.0, [B_KT])
        memset("pool", V, 0.0, [B_V])
        memset("pool", V[:, 1:9, :, 64:65], 1.0, [B_V])
        memset("pool", carry[:], 0.0, [B_carry])

        WU, WZS, WQ, WK, WV, WZA = 0, 512, 1024, 1536, 1664, 1792
        WOUT, WGATE, WPROJ, WBG = 0, 8192, 16384, 18432

        def win(kt, c0, n):
            return wbuf[:, kt * 2304 + c0: kt * 2304 + c0 + n]

        evac_rr = [0]

        def evac_eng():
            evac_rr[0] += 1
            return "act" if evac_rr[0] % 2 else "dve"

        xslot = [0]

        p1slot = {}
        p1_early = set()

        def p1_A(b, t):
            xl = xslot[0] % 4; xslot[0] += 1
            sl = t % 2
            p1slot[(b, t)] = sl
            fw.dma(xs[xl], x_d[b * 1024 + t * 128:b * 1024 + (t + 1) * 128, :], d_xs[xl], writes=[B_xs[xl]])
            act(hn[sl], xs[xl], AF.Square, [B_xs[xl]], [B_hn[sl], B_ss], accum=ss[:, t:t + 1])
            ts("dve", rstd[:, t:t + 1], ss[:, t:t + 1], 1.0 / 1024.0, EPS, ALU.mult, ALU.add, [B_ss], [B_rstd])
            tt("pool", rstd[:, t:t + 1], rstd[:, t:t + 1], mhalf[:, 0:1], ALU.pow, [B_rstd, B_mhalf], [B_rstd])
            ts("dve", hn[sl], xs[xl], rstd[:, t:t + 1], None, ALU.mult, None, [B_xs[xl], B_rstd], [B_hn[sl]])

        for blk in range(NBLK):
            half = blk % 2
            if half == 0:
                memset("pool", KT[:, :, 0:128], 0.0, [B_KT])
                memset("pool", V[:, 0, :, :], 0.0, [B_V])
                memset("pool", carry[:], 0.0, [B_carry])
            else:
                cp("pool", KT[:, :, 0:128], KT[:, :, 1024:1152], [B_KT], [B_KT])
                cp("pool", V[:, 0, :, :], V[:, 8, :, :], [B_V], [B_V])
            def p1_B(t):
                sl = p1slot[(blk, t)]
                pst, Bp = (psT, B_psT) if t % 2 == 0 else (psS16, B_psS)
                for kt in range(8):
                    tr(pst[:, kt * 128:(kt + 1) * 128], hn[sl][:, kt * 128:(kt + 1) * 128], identb[:], [B_hn[sl], S0], [Bp])
                cp("act" if t % 2 == 0 else "dve", bigT[:, :, t * 128:(t + 1) * 128], pst[:].rearrange("p (a b) -> p a b", a=8), [Bp], [B_bigTh[0], B_bigTh[1]])
            if (blk, 0) not in p1_early:
                p1_A(blk, 0)
            for t in range(8):
                if t + 1 < 8 and (blk, t + 1) not in p1_early:
                    p1_A(blk, t + 1)
                if blk == 0:
                    c0 = 2304 * t
                    fw.dma(scr_in[:, c0:c0 + 2304], wbuf[:, c0:c0 + 2304], d_scr, reads=[B_wb0 if t < 3 else B_wbuf], writes=[B_scr])
                if blk > 0 and t < 5:
                    c0 = 6912 + 2304 * t
                    fw.dma(wbuf[:, c0:c0 + 2304], scr_in[:, c0:c0 + 2304], d_wb, reads=[B_scr], writes=[B_wbuf])
                p1_B(t)
            rr = [0]

            def bank():
                rr[0] += 1
                return rr[0] % 4
            for t in range(8):
                bk = bank()
                for kt in range(8):
                    mm(psA[:, bk, :], bigT[:, kt, t::8], win(kt, WU, 512), kt == 0, kt == 7, [B_bigTh[0], B_bigTh[1], (B_wb0 if kt < 3 else B_wbuf)], [B_psA[bk]])
                uv = uSM.rearrange("p (g t c) -> p g t c", g=32, t=8)
                cp(evac_eng(), uv[:, :, t, :], psA[:, bk, :].rearrange("p (g c) -> p g c", g=32), [B_psA[bk]], [B_uSM])
            uv2 = uSM.rearrange("p (g x) -> p g x", g=32)
            for g4 in range(8):
                pst, Bp = (psT, B_psT) if g4 % 2 == 0 else (psS16, B_psS)
                for j in range(4):
                    g = 4 * g4 + j
                    tr(pst[:, j * 128:(j + 1) * 128], uv2[:, g, :], identb[:], [B_uSM, S0], [Bp])
                cp(evac_eng(), UgT[:, 4 * g4:4 * g4 + 4, :], pst[:, 0:512].rearrange("p (a b) -> p a b", a=4), [Bp], [B_UgT])
            for g4 in range(8):
                pbk, Bp = (psB[:, 0, :], B_psB[0]) if g4 % 2 == 0 else (psB[:, 1, :], B_psB[1])
                for j in range(4):
                    g = 4 * g4 + j
                    mm(pbk[:, j * 128:(j + 1) * 128], RT[:, g, :], UgT[:, g, :], True, True, [S0, B_UgT], [Bp])
                cp(evac_eng(), rb[:, 4 * g4:4 * g4 + 4, :], pbk.rearrange("p (a b) -> p a b", a=4), [Bp], [B_rb])
            SE = "dve"

            r2 = tmpA.bitcast(BF16).rearrange("p (g c) -> p g c", g=32)
            tf = tmpg.rearrange("p (k g c) -> p k g c", k=2, g=32)

            def cmul_A(eng, dst, src_lohi, addend, ARt, AIt, Bsrc, Bdst, w=16):
                ARb = ARt[:].unsqueeze(2).to_broadcast([128, 32, w])
                t0 = tf[:, 0, :, 0:w]
                tt(eng, t0, src_lohi, ARb, ALU.mult, Bsrc + [S0], [B_tmpg])
                tt(eng, tf[0:64, 1, :, 0:w], src_lohi[64:128], AIt[64:128, :].unsqueeze(2).to_broadcast([64, 32, w]), ALU.mult, Bsrc + [S0], [B_tmpg])
                tt(eng, tf[64:128, 1, :, 0:w], src_lohi[0:64], AIt[0:64, :].unsqueeze(2).to_broadcast([64, 32, w]), ALU.mult, Bsrc + [S0], [B_tmpg])
                tt(eng, t0, t0, tf[:, 1, :, 0:w], ALU.add, [B_tmpg], [B_tmpg])
                tt(eng, dst, t0, addend, ALU.add, [B_tmpg] + Bsrc, Bdst)

            r4 = E.rearrange("p a b c -> p (a b c)")[:, 0:1024].rearrange("p (g c) -> p g c", g=32)
            B_r4 = B_Ek[0]

            r8 = E.rearrange("p a b c -> p (a b c)")[:, 1024:1536].rearrange("p (g c) -> p g c", g=32)
            r16 = E.rearrange("p a b c -> p (a b c)")[:, 1536:1792].rearrange("p (g c) -> p g c", g=32)

            def scan_gen():
                cp(SE, Xbf[:, :, 0], carry[:], [B_carry], [B_Xbf])
                for ch in range(4):
                    c0 = 32 * ch
                    cmul_A(SE, r2[:, :, 16 * ch:16 * ch + 16], rb[:, :, c0:c0 + 32:2], rb[:, :, c0 + 1:c0 + 32:2], AR, AI, [B_rb], [B_tmpA])
                    yield
                for ch in range(2):
                    c0 = 32 * ch
                    cmul_A(SE, r4[:, :, 16 * ch:16 * ch + 16], r2[:, :, c0:c0 + 32:2], r2[:, :, c0 + 1:c0 + 32:2], AR2, AI2, [B_tmpA], [B_r4])
                    yield
                cmul_A(SE, r8[:, :, 0:16], r4[:, :, 0:32:2], r4[:, :, 1:32:2], AR4, AI4, [B_r4], [B_r4])
                yield
                cmul_A(SE, r16[:, :, 0:8], r8[:, :, 0:16:2], r8[:, :, 1:16:2], AR8, AI8, [B_r4], [B_r4], w=8)
                yield
                rg = ring[0]; Brg = B_ring[0]
                for cc in range(8):
                    if cc == 0:
                        xp = carry[:]; Bxp = B_carry
                    else:
                        xp = rg[:, :, cc - 1]; Bxp = Brg
                    tt(SE, st1, xp, AR16[:], ALU.mult, [Bxp, S0], [B_st1])
                    tt(SE, st2[0:64, :], xp[64:128, :], AI16[64:128, :], ALU.mult, [Bxp, S0], [B_st2])
                    tt(SE, st2[64:128, :], xp[0:64, :], AI16[0:64, :], ALU.mult, [Bxp, S0], [B_st2])
                    tt(SE, st1, st1, r16[:, :, cc], ALU.add, [B_st1, B_r4], [B_st1])
                    tt(SE, rg[:, :, cc], st1, st2, ALU.add, [B_st1, B_st2], [Brg])
                    yield
                cp(SE, Xbf[:, :, 16:128:16], rg[:, :, 0:7], [Brg], [B_Xbf])
                cp(SE, carry[:], rg[:, :, 7], [Brg], [B_carry])
                cmul_A("pool", Xbf[:, :, 8:128:16], Xbf[:, :, 0:128:16], r8[:, :, 0:16:2], AR8, AI8, [B_Xbf, B_r4], [B_Xbf], w=8)
                yield
                cmul_A("pool", Xbf[:, :, 4:128:8], Xbf[:, :, 0:128:8], r4[:, :, 0:32:2], AR4, AI4, [B_Xbf, B_r4], [B_Xbf])
                yield
                for e0 in (0, 64):
                    cmul_A("pool", Xbf[:, :, e0 + 2:e0 + 64:4], Xbf[:, :, e0:e0 + 64:4], r2[:, :, e0 // 2:e0 // 2 + 32:2], AR2, AI2, [B_Xbf, B_tmpA], [B_Xbf])
                    yield
                for e1 in (0, 32, 64, 96):
                    cmul_A("pool", Xbf[:, :, e1 + 1:e1 + 32:2], Xbf[:, :, e1:e1 + 32:2], rb[:, :, e1:e1 + 32:2], AR, AI, [B_Xbf, B_rb], [B_Xbf])
                    yield
            sg = scan_gen()

            def scan_advance(k):
                for _ in range(k):
                    if next(sg, "done") == "done":
                        break
            def zs_proj(t):
                for kt in range(8):
                    mm(psS[:], bigT[:, kt, t::8], win(kt, WZS, 512), kt == 0, kt == 7, [B_bigTh[0], B_bigTh[1], (B_wb0 if kt < 3 else B_wbuf)], [B_psS])
                act(zs[:, t, :], psS[:], AF.Copy, [B_psS], [B_zs])

            def za_proj(qb):
                for kt in range(8):
                    mm(psS[:], bigT[:, kt, qb * 128:(qb + 1) * 128], win(kt, WZA, 512), kt == 0, kt == 7, [B_bigTh[0], B_bigTh[1], (B_wb0 if kt < 3 else B_wbuf)], [B_psS])
                act(sig[qb % 2], psS[:], AF.Tanh, [B_psS], [B_sig[qb % 2]], scale=0.5)
                act(za[:, qb, :], psS[:], AF.Copy, [B_psS], [B_za], scale=0.5)
            for j in range(4):
                for n in range(2):
                    bk = bank()
                    for kt in range(8):
                        mm(psA[:, bk, :], win(kt, WQ + j * 128, 128), bigT[:, kt, n * 512:(n + 1) * 512], kt == 0, kt == 7, [B_bigTh[0], B_bigTh[1], (B_wb0 if kt < 3 else B_wbuf)], [B_psA[bk]])
                    cp("act", QT[:, j, n * 512:(n + 1) * 512], psA[:, bk, :], [B_psA[bk]], [B_QT])
                    scan_advance(1)
            for n in range(2):
                bk = bank()
                for kt in range(8):
                    mm(psA[:, bk, :], win(kt, WK, 128), bigT[:, kt, n * 512:(n + 1) * 512], kt == 0, kt == 7, [B_bigTh[0], B_bigTh[1], (B_wb0 if kt < 3 else B_wbuf)], [B_psA[bk]])
                for kv in range(2):
                    for e in range(2):
                        cp("act", KT[e * 64:(e + 1) * 64, kv * 2 + e, 128 + n * 512:128 + (n + 1) * 512],
                           psA[kv * 64:(kv + 1) * 64, bk, :], [B_psA[bk]], [B_KT])
                scan_advance(2)
            for qh in range(2):
                bk = bank()
                for q4 in range(4):
                    qb = qh * 4 + q4
                    for kt in range(8):
                        mm(psA[:, bk, q4 * 128:(q4 + 1) * 128], bigT[:, kt, qb * 128:(qb + 1) * 128], win(kt, WV, 128), kt == 0, kt == 7, [B_bigTh[0], B_bigTh[1], (B_wb0 if kt < 3 else B_wbuf)], [B_psA[bk]])
                for q4 in range(4):
                    qb = qh * 4 + q4
                    cp("act", V[:, 1 + qb, :, 0:64], psA[:, bk, q4 * 128:(q4 + 1) * 128].rearrange("p (a b) -> p a b", a=2), [B_psA[bk]], [B_V])
                scan_advance(2)
            den8v = den8.rearrange("p (a h) -> p a h", a=2)
            rden8v = rden8.rearrange("p (a h) -> p a h", a=2)
            esinkv = esink[:].rearrange("p (a h) -> p a h", a=2)

            def p4_S(qb, kv):
                for kb in range(2):
                    bk = kv * 2 + kb
                    for e in range(2):
                        for jj in range(2):
                            hl = 2 * jj + e
                            mm(psA[:, bk, hl * 128:(hl + 1) * 128], identb[:], MD[:, kb, 4 * kv + hl, :], True, False, [S0], [B_psA[bk]])
                            mm(psA[:, bk, hl * 128:(hl + 1) * 128],
                               KT[:, kv * 2 + e, (qb + kb) * 128:(qb + kb + 1) * 128],
                               QT[:, 2 * kv + jj, qb * 128:(qb + 1) * 128], False, True, [B_KT, B_QT], [B_psA[bk]])
                    act(PT[:, kb, 4 * kv:4 * kv + 4, :], psA[:, bk, :].rearrange("p (a b) -> p a b", a=4), AF.Exp, [B_psA[bk]], [B_PTk[kv]], scale=0.125)

            def p4_V(qb, kv):
                for hl in range(4):
                    h_ = 4 * kv + hl
                    for kb in range(2):
                        mm(psB[:, kv, hl * 65:hl * 65 + 65], PT[:, kb, h_, :], V[:, qb + kb, kv, :], kb == 0, kb == 1, [B_PTk[kv], B_V], [B_psB[kv]])
                pv = psB[:, kv, 0:260].rearrange("p (h d) -> p h d", h=4)
                tt("dve", den8v[:, kv, :], pv[:, :, 64], esinkv[:, kv, :], ALU.add, [B_psB[kv], S0], [B_denk[kv]])
                fw.op("dve", lambda h, kv=kv: h.reciprocal(out=rden8v[:, kv, :], in_=den8v[:, kv, :]), [B_denk[kv]], [B_denk[kv]])
                tt("dve", ao[:, 4 * kv:4 * kv + 4, :], pv[:, :, 0:64], rden8v[:, kv, :].unsqueeze(2).to_broadcast([128, 4, 64]), ALU.mult, [B_psB[kv], B_denk[kv]], [B_aok[kv]])

            def p4_F(qb):
                stt(za[:, qb, :], sig[qb % 2], 1.0, za[:, qb, :], ALU.add, ALU.mult, [B_sig[qb % 2], B_za], [B_za])
                tt("dve", za[:, qb, :], ao.rearrange("p h d -> p (h d)"), za[:, qb, :], ALU.mult, [B_aok[0], B_aok[1], B_za], [B_za])

            def p4_T(qb):
                pst, Bp = (psT, B_psT) if qb % 2 == 0 else (psS16, B_psS)
                for f in range(4):
                    tr(pst[:, f * 128:(f + 1) * 128], za[:, qb, f * 128:(f + 1) * 128], identb[:], [B_za, S0], [Bp])
                cp("act" if qb % 2 == 0 else "dve", bigT[:, 4:8, qb * 128:(qb + 1) * 128], pst[:, 0:512].rearrange("p (a b) -> p a b", a=4), [Bp], [B_bigTh[1]])
            units = [(qb, kv) for qb in range(8) for kv in range(2)]
            for i in range(len(units) + 2):
                if 0 <= i - 2 < len(units):
                    qb, kv = units[i - 2]
                    p4_V(qb, kv)
                    if kv == 1:
                        p4_F(qb)
                if i < len(units):
                    p4_S(*units[i])
                    if i % 2 == 0:
                        za_proj(i // 2)
                    else:
                        zs_proj(i // 2)
                    scan_advance(2)
            scan_advance(1000)
            act(zs.rearrange("p a b -> p (a b)"), zs.rearrange("p a b -> p (a b)"), AF.Silu, [B_zs], [B_zs])
            for qb in range(8):
                p4_T(qb)
            fw.dma(wbuf[:, :], scr_p5[:, :], d_wb, reads=[B_scr], writes=[B_wbuf, B_wb0])
            scan_advance(1000)
            gv = uSM.rearrange("p (t g c) -> p t g c", t=8, g=32)
            for g4 in range(8):
                pbk, Bp = (psS[:], B_psS) if g4 % 2 == 0 else (psA[:, 3, :], B_psA[3])
                for j in range(4):
                    g = 4 * g4 + j
                    mm(pbk[:, j * 128:(j + 1) * 128], UgT[:, g, :], TT[:, g, :], True, False, [B_UgT, S0], [Bp])
                    mm(pbk[:, j * 128:(j + 1) * 128], Xbf[:, g, :], OTb[:, g, :], False, True, [B_Xbf, S0], [Bp])
                act(gv[:, :, 4 * g4:4 * g4 + 4, :], pbk.rearrange("p (j t c) -> p t j c", j=4, t=8), AF.Gelu_apprx_tanh, [Bp], [B_uSM])
            gv2 = uSM.rearrange("p (t f) -> p t f", t=8)

            def glu_s1(t):
                k = t % 2
                for f in range(4):
                    tr(psT[:, f * 128:(f + 1) * 128], gv2[:, t, f * 128:(f + 1) * 128], identb[:], [B_uSM, S0], [B_psT])
                cp("dve", gTs[k], psT[:, 0:512].rearrange("p (a b) -> p a b", a=4), [B_psT], [B_gTs[k]])

            def glu_s2(t):
                k = t % 2
                for f in range(4):
                    mm(psB[:, k, :], gTs[k][:, f, :], wglu[:, f, :], f == 0, False, [B_gTs[k], B_wglu], [B_psB[k]])
                mm(psB[:, k, :], onesb[:], bglu[:], False, True, [S0], [B_psB[k]])
                act(sig[k], psB[:, k, :], AF.Sigmoid, [B_psB[k]], [B_sig[k]])
                tt("dve", so[k], gv2[:, t, :], zs[:, t, :], ALU.mult, [B_uSM, B_zs], [B_so[k]])
                tt("dve", so[k], so[k], sig[k], ALU.mult, [B_so[k], B_sig[k]], [B_so[k]])

            def glu_s3(t):
                k = t % 2
                pst, Bp = (psS16, B_psS) if k == 0 else (psA3_16, B_psA[3])
                for f in range(4):
                    tr(pst[:, f * 128:(f + 1) * 128], so[k][:, f * 128:(f + 1) * 128], identb[:], [B_so[k], S0], [Bp])
                cp("act" if k == 0 else "dve", bigT[:, 0:4, t * 128:(t + 1) * 128], pst[:, 0:512].rearrange("p (a b) -> p a b", a=4), [Bp], [B_bigTh[0]])
            for t in range(10):
                if t < 8:
                    glu_s1(t)
                if 0 <= t - 1 < 8:
                    glu_s2(t - 1)
                if 0 <= t - 2 < 8:
                    glu_s3(t - 2)
            p5slot = {}

            def p5_A(t):
                xl = xslot[0] % 4; xslot[0] += 1
                sl = t % 2
                p5slot[t] = (sl, xl)
                fw.dma(xs[xl], xv[blk, :, t, :], d_xs[xl], reads=[], writes=[B_xs[xl]])
                fw.dma(pslab[sl], pv[blk, :, t, :], d_ps[sl], writes=[B_ps[sl]])
                for n in range(2):
                    for kt in range(8):
                        mT = bigT[:, kt, t * 128:(t + 1) * 128] if kt < 4 else bigT[:, kt, t::8]
                        mm(psA[:, n, :], mT, wbuf[:, WOUT + kt * 1024 + n * 512:WOUT + kt * 1024 + (n + 1) * 512], kt == 0, kt == 7, [B_bigTh[0], B_bigTh[1]] + ([B_wb0] if kt <= 5 else [B_wb0, B_wbuf]), [B_psA[n]])

            def p5_A2(t):
                sl, xl = p5slot[t]
                act(hn[sl], psA[:, 0:2, :].rearrange("p a b -> p (a b)"), AF.Square, [B_psA[0], B_psA[1]], [B_hn[sl], B_ss2], accum=ss2[:, t:t + 1])
                ts("dve", rstd2[:, t:t + 1], ss2[:, t:t + 1], 1.0 / 1024.0, EPS, ALU.mult, ALU.add, [B_ss2], [B_rstd2])
                tt("pool", rstd2[:, t:t + 1], rstd2[:, t:t + 1], mhalf[:, 0:1], ALU.pow, [B_rstd2, B_mhalf], [B_rstd2])
                stt(tmpA, psA[:, 0:2, :].rearrange("p a b -> p (a b)"), rstd2[:, t:t + 1], gpost[:], ALU.mult, ALU.mult, [B_psA[0], B_psA[1], B_rstd2, B_gpost], [B_tmpA])
                tt("dve", xs[xl], xs[xl], tmpA, ALU.add, [B_xs[xl], B_tmpA], [B_xs[xl]])
                cp("act", hn[sl], xs[xl], [B_xs[xl]], [B_hn[sl]])
                cp("pool", pb[sl], pslab[sl], [B_ps[sl]], [B_pb[sl]])

            def p5_B(t):
                sl, xl = p5slot[t]
                for kt in range(8):
                    tr(psT[:, kt * 128:(kt + 1) * 128], hn[sl][:, kt * 128:(kt + 1) * 128], identb[:], [B_hn[sl], S0], [B_psT])
                cp("dve", hTs, psT[:].rearrange("p (a b) -> p a b", a=8), [B_psT], [B_hTs])

            def p5_B2(t):
                sl, xl = p5slot[t]
                for n in range(2):
                    for kt in range(8):
                        mm(psA[:, 2 + n, :], hTs[:, kt, :], wbuf[:, WGATE + kt * 1024 + n * 512:WGATE + kt * 1024 + (n + 1) * 512], kt == 0, False, [B_hTs, B_wbuf], [B_psA[2 + n]])
                    mm(psA[:, 2 + n, :], onesb[:], wbuf[:, WBG + n * 512:WBG + (n + 1) * 512], False, True, [S0, B_wbuf], [B_psA[2 + n]])
                act(tmpg, psA[:, 2:4, :].rearrange("p a b -> p (a b)"), AF.Sigmoid, [B_psA[2], B_psA[3]], [B_tmpg])
                for kt in range(2):
                    tr(psT[:, kt * 128:(kt + 1) * 128], pb[sl][:, kt * 128:(kt + 1) * 128], identb[:], [B_pb[sl], S0], [B_psT])
                cp("dve", pTs, psT[:, 0:256].rearrange("p (a b) -> p a b", a=2), [B_psT], [B_pTs])
                for n in range(2):
                    for kt in range(2):
                        mm(psB[:, n, :], pTs[:, kt, :], wbuf[:, WPROJ + kt * 1024 + n * 512:WPROJ + kt * 1024 + (n + 1) * 512], kt == 0, kt == 1, [B_pTs, B_wbuf], [B_psB[n]])
                tt("dve", tmpg, tmpg, psB[:, 0:2, :].rearrange("p a b -> p (a b)"), ALU.mult, [B_tmpg, B_psB[0], B_psB[1]], [B_tmpg])
                tt("dve", xs[xl], xs[xl], tmpg, ALU.add, [B_xs[xl], B_tmpg], [B_xs[xl]])
                fw.dma(ov[blk, :, t, :], xs[xl], d_out[xl], reads=[B_xs[xl]])

            p5_A(0)
            p5_A2(0)
            for t in range(8):
                if t + 1 < 8:
                    p5_A(t + 1)
                    if t + 1 == 7 and blk + 1 < NBLK:
                        fw.dma(wbuf[:, 0:6912], scr_in[:, 0:6912], d_wb0, reads=[B_scr], writes=[B_wb0])
                p5_B(t)
                if t + 1 < 8:
                    p5_A2(t + 1)
                if t == 7 and blk + 1 < NBLK:
                    for tt_ in (0, 1):
                        p1_A(blk + 1, tt_)
                        p1_early.add((blk + 1, tt_))
                p5_B2(t)

        if taps:
            d_tap = fw.dsem("tap")
            tapsrc = {"TT": (TT, B_res), "RT": (RT, B_res), "OTb": (OTb, B_res), "AR": (AR, B_res), "AI": (AI, B_res),
                      "MD": (MD, B_res), "bigT": (bigT, B_bigT), "uSM": (uSM, B_uSM), "zs": (zs, B_zs), "za": (za, B_za),
                      "QT": (QT, B_QT), "KT": (KT, B_KT), "Xbf": (Xbf, B_Xbf), "rb": (rb, B_rb), "UgT": (UgT, B_UgT)}
            fw.barrier()
            tstage = arena.carve([128, 4096], F32) if False else None
            for name, shape in taps.items():
                src, bsrc = tapsrc[name]
                n = 1
                for s_ in shape[1:]:
                    n *= s_
                flat = src if len(src.shape) == 2 else (src.rearrange("p a b -> p (a b)") if len(src.shape) == 3 else src.rearrange("p a b c -> p (a b c)"))
                if src.dtype == F32:
                    fw.dma(tap_d[name][:, :], flat, d_tap, reads=[bsrc])
                else:
                    for c0 in range(0, n, 1024):
                        c1 = min(n, c0 + 1024)
                        cp("dve", xs[0][:, 0:c1 - c0], flat[:, c0:c1], [bsrc, B_xs[0]], [B_xs[0]])
                        fw.dma(tap_d[name][:, c0:c1], xs[0][:, 0:c1 - c0], d_tap, reads=[B_xs[0]])
                        fw.barrier()
        fw.barrier()
        fw.emit_all()
    return nc


def _prep_inputs(x, p, pre_norm_g, w_in, ssm_lam_re, ssm_lam_im, ssm_log_step, ssm_b_re, ssm_b_im,
                 ssm_c_re, ssm_c_im, ssm_d, ssm_w_glu, ssm_b_glu, attn_sinks, w_out, post_norm_g,
                 pl_w_proj, pl_w_gate, pl_b_gate):
    f = np.float32

    def kt_layout(w):
        K, N = w.shape
        return np.ascontiguousarray(w.reshape(K // 128, 128, N).transpose(1, 0, 2).reshape(128, (K // 128) * N)).astype(f)
    shared = {}
    shared["w_in_r"] = kt_layout(np.asarray(w_in[0]))
    bg = np.zeros((128, 1024), f)
    bg[0, :] = np.asarray(pl_b_gate[0])
    shared["w5"] = np.ascontiguousarray(np.concatenate(
        [kt_layout(np.asarray(w_out[0])), kt_layout(np.asarray(pl_w_gate[0])), kt_layout(np.asarray(pl_w_proj[0])), bg], axis=1))
    shared["wglu_r"] = kt_layout(np.asarray(ssm_w_glu[0]))
    bgl = np.zeros((128, 512), f)
    bgl[0, :] = np.asarray(ssm_b_glu[0])
    shared["bglu_pad"] = bgl
    shared["gcol"] = np.ascontiguousarray(np.asarray(pre_norm_g[0]).reshape(8, 128).T).astype(f)
    shared["gpost_t"] = np.ascontiguousarray(np.broadcast_to(np.asarray(post_norm_g[0])[None, :], (128, 1024))).astype(f)
    lre = np.asarray(ssm_lam_re[0]).T
    lim = np.asarray(ssm_lam_im[0]).T
    shared["lamre2"] = np.ascontiguousarray(np.concatenate([lre, lre], 0)).astype(f)
    shared["lamim2"] = np.ascontiguousarray(np.concatenate([lim, lim], 0)).astype(f)
    shared["lstep"] = np.ascontiguousarray(np.broadcast_to(np.asarray(ssm_log_step[0])[None, :], (128, 32))).astype(f)
    bre = np.asarray(ssm_b_re[0]).transpose(1, 0, 2).reshape(64, 512)
    bim = np.asarray(ssm_b_im[0]).transpose(1, 0, 2).reshape(64, 512)
    shared["bA"] = np.ascontiguousarray(np.concatenate([bre, bim], 0)).astype(f)
    shared["bB"] = np.ascontiguousarray(np.concatenate([bim, bre], 0)).astype(f)
    cre = np.asarray(ssm_c_re[0]).transpose(2, 0, 1).reshape(64, 512)
    cim = np.asarray(ssm_c_im[0]).transpose(2, 0, 1).reshape(64, 512)
    shared["cA"] = np.ascontiguousarray(np.concatenate([cre, cim], 0)).astype(f)
    shared["cB"] = np.ascontiguousarray(np.concatenate([cim, cre], 0)).astype(f)
    d = np.asarray(ssm_d[0]).reshape(32, 16)
    shared["dcol"] = np.ascontiguousarray(np.tile(d.T, (8, 1))).astype(f)
    shared["sinks_b"] = np.ascontiguousarray(np.broadcast_to(np.asarray(attn_sinks[0])[None, :], (128, 8))).astype(f)
    shared["identf"] = np.eye(128, dtype=f)
    sq_s = np.arange(128) // 16
    shared["mcausal"] = (sq_s[:, None] <= sq_s[None, :]).astype(f)
    s_idx = np.arange(128)[:, None]
    q_idx = np.arange(128)[None, :]
    prev = np.where(s_idx > q_idx, (q_idx + 128 - s_idx).astype(f), f(BIG))
    cur = np.where(s_idx <= q_idx, (q_idx - s_idx).astype(f), f(BIG))
    shared["negtab"] = np.ascontiguousarray(np.concatenate([prev, cur], 1)).astype(f)
    sg = np.ones((128, 1), f)
    sg[0:64] = -1.0
    shared["sgn"] = sg
    xs_ = np.asarray(x)
    ps_ = np.asarray(p[0])
    in_maps = []
    for c in range(NCORES):
        m = dict(shared)
        m["x"] = np.ascontiguousarray(xs_[2 * c:2 * c + 2].reshape(4096, 1024)).astype(f)
        m["p"] = np.ascontiguousarray(ps_[2 * c:2 * c + 2].reshape(4096, 256)).astype(f)
        in_maps.append(m)
    return in_maps


_NC_CACHE = {}


def kernel(**inputs):
    in_maps = _prep_inputs(**inputs)
    if "nc" not in _NC_CACHE:
        _NC_CACHE["nc"] = build_nc()
    nc = _NC_CACHE["nc"]
    res = run_bass_kernel_spmd(nc, in_maps, core_ids=list(range(NCORES)))
    outs = [np.asarray(r["out"]).reshape(2, 2048, 1024) for r in res.results]
    return np.concatenate(outs, axis=0).astype(np.float32)
```
